# Optimizing a Trainium2 kernel written in Bass

```python
import math
import jax, jax.numpy as jnp
from jax import lax
import numpy as np

D_MODEL = 2048
BATCH = 4
SEQ = 4096
DEPTH = 4

GRID_W = 64
HEAD_DIM = 128
N_Q_HEADS = 16
N_KV_HEADS = 4
GROUP = N_Q_HEADS // N_KV_HEADS
ATTN_WIDTH = N_Q_HEADS * HEAD_DIM
KV_WIDTH = N_KV_HEADS * HEAD_DIM
CONV_WIDTH = D_MODEL
CONV_K = 3
D_FF = 4 * D_MODEL
Q_BLOCK = 128
ROPE_THETA = 10000.0
RMS_EPS = 1e-6
AXIS_DIM = HEAD_DIM // 2
N_FREQ = AXIS_DIM // 2
IN_SPLITS = (CONV_WIDTH, CONV_WIDTH, CONV_WIDTH, ATTN_WIDTH, KV_WIDTH, KV_WIDTH, D_MODEL, D_MODEL)
IN_WIDTH = sum(IN_SPLITS)
IN_OFFSETS = tuple(int(o) for o in np.cumsum(IN_SPLITS)[:-1])

kernel_name = 'hybrid_shortconv_gqa_axial_encoder'


def rmsnorm(x, g):
    xf = x.astype(jnp.float32)
    y = xf * lax.rsqrt(jnp.mean(xf * xf, axis=-1, keepdims=True) + RMS_EPS)
    return (y * g.astype(jnp.float32)).astype(x.dtype)


def axial_rope_tables(seq_len):
    rows = seq_len // GRID_W
    row_idx = jnp.repeat(jnp.arange(rows, dtype=jnp.int32), GRID_W)
    col_idx = jnp.tile(jnp.arange(GRID_W, dtype=jnp.int32), rows)
    inv_freq = ROPE_THETA ** (-jnp.arange(0, AXIS_DIM, 2, dtype=jnp.float32) / AXIS_DIM)
    ang = jnp.stack([row_idx.astype(jnp.float32)[:, None] * inv_freq,
                     col_idx.astype(jnp.float32)[:, None] * inv_freq], axis=1)
    return jnp.cos(ang), jnp.sin(ang)


def apply_axial_rope(x, cos, sin):
    b, s, h, _ = x.shape
    xr = x.astype(jnp.float32).reshape(b, s, h, 2, 2, N_FREQ)
    x1, x2 = xr[..., 0, :], xr[..., 1, :]
    c, sn = cos[None, :, None], sin[None, :, None]
    out = jnp.stack([x1 * c - x2 * sn, x2 * c + x1 * sn], axis=-2)
    return out.reshape(b, s, h, HEAD_DIM).astype(x.dtype)


def short_conv_mixer(conv_b, conv_c, h_in, w_conv, w_out):
    u = conv_c * h_in
    up = jnp.pad(u, ((0, 0), (1, 1), (0, 0)))
    conv = w_conv[0] * up[:, :-2] + w_conv[1] * up[:, 1:-1] + w_conv[2] * up[:, 2:]
    return (conv_b * conv) @ w_out


def block_gqa(q, k, v):
    b, s, _, _ = q.shape
    nb = s // Q_BLOCK
    scale = 1.0 / math.sqrt(HEAD_DIM)
    qb = (q * scale).reshape(b, nb, Q_BLOCK, N_KV_HEADS, GROUP, HEAD_DIM).transpose(1, 0, 2, 3, 4, 5)

    def one_block(q_blk):
        scores = jnp.einsum('bqkgd,bskd->bkgqs', q_blk, k).astype(jnp.float32)
        p = jax.nn.softmax(scores, axis=-1).astype(v.dtype)
        return jnp.einsum('bkgqs,bskd->bqkgd', p, v)

    o = lax.map(one_block, qb)
    return o.transpose(1, 0, 2, 3, 4, 5).reshape(b, s, ATTN_WIDTH)


def setup_inputs(seed: int = 0) -> dict:
    key = jax.random.key(seed)
    ks = jax.random.split(key, 16)
    f32 = jnp.float32
    nrm = lambda k, shape, scale: jax.random.normal(k, shape, f32) * scale
    gain = lambda k, shape: 1.0 + 0.02 * jax.random.normal(k, shape, f32)
    return {
        'x': jax.random.normal(ks[0], (BATCH, SEQ, D_MODEL), f32),
        'norm_mix_pre': gain(ks[1], (DEPTH, D_MODEL)),
        'w_in': nrm(ks[2], (DEPTH, D_MODEL, IN_WIDTH), D_MODEL ** -0.5),
        'gate_bias': nrm(ks[3], (DEPTH, 2 * D_MODEL), 0.01),
        'conv_w': nrm(ks[4], (DEPTH, CONV_K, CONV_WIDTH), CONV_K ** -0.5),
        'q_norm': gain(ks[5], (DEPTH, HEAD_DIM)),
        'k_norm': gain(ks[6], (DEPTH, HEAD_DIM)),
        'w_out_conv': nrm(ks[7], (DEPTH, CONV_WIDTH, D_MODEL), CONV_WIDTH ** -0.5),
        'w_out_attn': nrm(ks[8], (DEPTH, ATTN_WIDTH, D_MODEL), ATTN_WIDTH ** -0.5),
        'w_merge': nrm(ks[9], (DEPTH, D_MODEL, D_MODEL), D_MODEL ** -0.5),
        'norm_mix_post': gain(ks[10], (DEPTH, D_MODEL)),
        'norm_mlp_pre': gain(ks[11], (DEPTH, D_MODEL)),
        'w_up': nrm(ks[12], (DEPTH, D_MODEL, D_FF), D_MODEL ** -0.5),
        'w_down': nrm(ks[13], (DEPTH, D_FF, D_MODEL), D_FF ** -0.5),
        'norm_mlp_post': gain(ks[14], (DEPTH, D_MODEL)),
    }


def reference(x, norm_mix_pre, w_in, gate_bias, conv_w, q_norm, k_norm, w_out_conv, w_out_attn,
              w_merge, norm_mix_post, norm_mlp_pre, w_up, w_down, norm_mlp_post):
    b, s, _ = x.shape
    cos, sin = axial_rope_tables(s)
    for l in range(DEPTH):
        h = rmsnorm(x, norm_mix_pre[l])
        z = h @ w_in[l]
        conv_b, conv_c, conv_in, q, k, v, g_a, g_b = jnp.split(z, IN_OFFSETS, axis=-1)

        y_a = short_conv_mixer(conv_b, conv_c, conv_in, conv_w[l], w_out_conv[l])

        q = apply_axial_rope(rmsnorm(q.reshape(b, s, N_Q_HEADS, HEAD_DIM), q_norm[l]), cos, sin)
        k = apply_axial_rope(rmsnorm(k.reshape(b, s, N_KV_HEADS, HEAD_DIM), k_norm[l]), cos, sin)
        v = v.reshape(b, s, N_KV_HEADS, HEAD_DIM)
        y_b = block_gqa(q, k, v) @ w_out_attn[l]

        gates = jax.nn.sigmoid(jnp.concatenate([g_a, g_b], axis=-1) + gate_bias[l])
        gate_a, gate_b = jnp.split(gates, 2, axis=-1)
        mixed = (gate_a * y_a + gate_b * y_b) @ w_merge[l]
        x = x + rmsnorm(mixed, norm_mix_post[l])

        h = rmsnorm(x, norm_mlp_pre[l])
        f = jnp.square(jax.nn.relu(h @ w_up[l])) @ w_down[l]
        x = x + rmsnorm(f, norm_mlp_post[l])
    return x
```

```python
import contextlib
import numpy as np
import ml_dtypes
import concourse.bass as bass
import concourse.mybir as mybir
from concourse.bass_utils import run_bass_kernel_spmd

F32 = mybir.dt.float32
BF16 = mybir.dt.bfloat16
AF = mybir.ActivationFunctionType
ALU = mybir.AluOpType

ENGS = ['pe', 'act', 'dve', 'pool', 'sp']
EIDX = {e: i for i, e in enumerate(ENGS)}

D = 2048
KC = 16
T = 512
NT = 4
TOK = 2048
HD = 128
NQ = 16
NKV = 4
DFF = 8192
FC = 64
NL = 4
NG = 280
NGS = NG // 8
NPART = 5
NGP = NG // NPART
DEBUG = False
DBG_ONLY = None
WGATHER = False
NPRM = 146
EPS = 1e-6
QSCALE = 1.0 / float(np.sqrt(128.0))
OFF_B, OFF_C, OFF_IN, OFF_Q, OFF_K, OFF_V, OFF_GA, OFF_GB = 0, 2048, 4096, 6144, 8192, 8704, 9216, 11264


class Op:
    __slots__ = ('eng', 'fn', 'idx', 'deps', 'dma_key', 'dma_val', 'dma_inc', 'signal',
                 'waits', 'vc', 'rank')


class Prog:
    def __init__(self):
        self.ops = {e: [] for e in ENGS}
        self.order = []
        self.last_w = {}
        self.readers = {}
        self.dma_count = {}
        self.last_dma = {}

    def add(self, eng, fn, reads=(), writes=(), dma_key=None, dma_inc=16):
        op = Op()
        op.eng = eng
        op.fn = fn
        op.idx = len(self.ops[eng])
        op.dma_key = dma_key
        op.dma_inc = dma_inc
        op.signal = False
        op.waits = []
        op.rank = None
        deps = {}
        for t in reads:
            w = self.last_w.get(t)
            if w is not None:
                deps[w] = True
        for t in writes:
            w = self.last_w.get(t)
            if w is not None and w not in deps:
                deps[w] = False
            for r in self.readers.get(t, ()):
                if r not in deps:
                    deps[r] = False
        if dma_key is not None:
            prev = self.last_dma.get(dma_key)
            if prev is not None and prev not in deps:
                deps[prev] = False
            self.last_dma[dma_key] = op
        op.deps = deps
        if dma_key is not None:
            self.dma_count[dma_key] = self.dma_count.get(dma_key, 0) + dma_inc
            op.dma_val = self.dma_count[dma_key]
        for t in reads:
            self.readers.setdefault(t, []).append(op)
        for t in writes:
            self.last_w[t] = op
            self.readers[t] = []
        self.ops[eng].append(op)
        self.order.append(op)
        return op

    def resolve(self):
        seen = {e: [-1] * 5 for e in ENGS}
        dseen = {e: {} for e in ENGS}
        for op in self.order:
            F = op.eng
            sF = seen[F]
            dF = dseen[F]
            best = {}
            for a, raw in op.deps.items():
                if a.dma_key is not None:
                    if dF.get(a.dma_key, 0) >= a.dma_val:
                        continue
                    op.waits.append(a)
                    dF[a.dma_key] = a.dma_val
                    for i in range(5):
                        if a.vc[i] > sF[i]:
                            sF[i] = a.vc[i]
                    continue
                E = a.eng
                if E == F and (E == 'pe' or not raw):
                    continue
                cur = best.get(E)
                if cur is None or a.idx > cur.idx:
                    best[E] = a
            for E, a in best.items():
                if sF[EIDX[E]] >= a.idx:
                    continue
                a.signal = True
                op.waits.append(a)
                for i in range(5):
                    if a.vc[i] > sF[i]:
                        sF[i] = a.vc[i]
            vc = list(sF)
            if op.dma_key is None:
                vc[EIDX[F]] = op.idx
            op.vc = vc
            op.deps = None

    def emit(self, nc, final_waits=()):
        self.resolve()
        for a in final_waits:
            if a.dma_key is None:
                a.signal = True
        for e in ENGS:
            n = 0
            for op in self.ops[e]:
                if op.dma_key is None and op.signal:
                    n += 1
                    op.rank = n
        sems = {}
        dsems = {}
        with contextlib.ExitStack() as st:
            for e in ENGS:
                sems[e] = st.enter_context(nc.semaphore('s_' + e))
            for i, k in enumerate(self.dma_count):
                dsems[k] = st.enter_context(nc.semaphore('d%d' % i))
            block = st.enter_context(nc.Block())

            def wait(eng, a):
                if a.dma_key is not None:
                    eng.wait_ge(dsems[a.dma_key], a.dma_val)
                else:
                    eng.wait_ge(sems[a.eng], a.rank)

            def run(engname, eng):
                for op in self.ops[engname]:
                    for a in op.waits:
                        wait(eng, a)
                    ins = op.fn(eng)
                    if op.dma_key is not None:
                        ins.then_inc(dsems[op.dma_key], op.dma_inc)
                    elif op.signal:
                        ins.then_inc(sems[engname], 1)
                if engname == 'sp':
                    for a in final_waits:
                        wait(eng, a)

            @block.tensor
            def _(eng):
                run('pe', eng)

            @block.scalar
            def _(eng):
                run('act', eng)

            @block.vector
            def _(eng):
                run('dve', eng)

            @block.gpsimd
            def _(eng):
                run('pool', eng)

            @block.sync
            def _(eng):
                run('sp', eng)


class DummyProg:
    def add(self, *a, **k):
        return None


class Builder:
    def __init__(self, nlayers, wseq=None):
        self.nl = nlayers
        self.wseq = wseq
        self.dry = wseq is None
        self.wrec = []
        self.wissued = 0
        self.dkc = {}
        nc = bass.Bass("TRN2", target_bir_lowering=False)
        self.nc = nc
        self.P = DummyProg() if self.dry else Prog()
        dt = nc.dram_tensor
        self.xin = dt("xin", [NT, 128, KC * T], F32, kind="ExternalInput").ap()
        self.out = dt("out", [NT, 128, KC * T], F32, kind="ExternalOutput").ap()
        if WGATHER:
            self.wsh = dt("wsh", [nlayers * NGS * 128, 2048], F32, kind="ExternalInput")
            self.wbn = dt("wbn", [nlayers * NGS * 128, 2048], F32)
            self.wg = [dt("wg%d" % i, [NGP * 128, 2048], F32) for i in range(nlayers * NPART)]
        else:
            self.wall = dt("wall", [nlayers * NG, 128, 2048], F32, kind="ExternalInput").ap()
            self.wbf = {l: dt("wbf%d" % l, [NG * 128, 2048], BF16) for l in range(1, nlayers)}
            self.cv_next = {l: 0 for l in range(1, nlayers)}
        self.prm_d = dt("prm", [128, nlayers * NPRM], F32, kind="ExternalInput").ap()
        self.ropec_d = dt("ropec", [128, TOK], F32, kind="ExternalInput").ap()
        self.ropes_d = dt("ropes", [128, TOK], F32, kind="ExternalInput").ap()
        self.rmat_d = dt("rmat", [128, 128], BF16, kind="ExternalInput").ap()
        self.mask_d = dt("mask", [128, 2], F32, kind="ExternalInput").ap()
        self.xmid = dt("xmid", [NT, 128, KC * T], F32).ap()
        self.xs = dt("xs", [NT, 128, KC * T], F32).ap()
        self.kin = dt("kin", [NKV * 128, TOK], BF16)
        self.kout = dt("kout", [2 * NKV * 128, TOK], BF16)
        self.vin = dt("vin", [NKV * 128, TOK], BF16)
        self.vout = dt("vout", [2 * NKV * 128, TOK], BF16)
        self.hin = dt("hin", [256, 16], BF16)
        self.hout = dt("hout", [512, 16], BF16)
        self.wctr = 0
        self.bank_rr = 0
        self.tf_rr = 0
        self.tb_rr = 0
        self.fin_ops = []
        self.dbg_names = []

    def dump(self, name, ap, toks, ncols, dtp):
        if not DEBUG or self.dry or (DBG_ONLY is not None and name not in DBG_ONLY):
            return
        d = self.nc.dram_tensor("dbg_" + name, [128, ncols], dtp, kind="ExternalOutput").ap()
        op = self.P.add('sp', lambda e: e.dma_start(out=d, in_=ap), reads=toks, writes=['dbg_' + name],
                        dma_key='dbg_' + name)
        self.fin_ops.append(op)
        self.dbg_names.append("dbg_" + name)

    def bank(self):
        b = self.bank_rr % 7
        self.bank_rr += 1
        return b

    def tf(self):
        i = self.tf_rr % len(self.tmpf)
        self.tf_rr += 1
        return i

    def tb(self):
        i = self.tb_rr % len(self.tmpb)
        self.tb_rr += 1
        return i

    def tfa(self, i, n=T, off=0):
        return self.tmpf[i][:, off:off + n]

    def tba(self, i, n=T, off=0):
        return self.tmpb[i][:, off:off + n]

    def psa(self, b, n=T, off=0):
        return self.ps[b][:, off:off + n]

    def dk(self, cls, n):
        i = self.dkc.get(cls, 0)
        self.dkc[cls] = i + 1
        return (cls, i % n)

    def wload(self, gidx):
        n = self.wctr
        self.wctr += 1
        nb = len(self.wb)
        if self.dry:
            self.wrec.append(gidx)
            return n % nb
        assert self.wseq[n] == gidx
        lim = min(n + nb - 1, len(self.wseq) - 1)
        while self.wissued <= lim:
            i = self.wissued
            self.wissued += 1
            sl = i % nb
            g = self.wseq[i]
            if WGATHER:
                part, gi = g // NGP, g % NGP
                src = self.wg[part].ap()[gi * 128:(gi + 1) * 128, :]
                rd = [('wg', part)]
            elif g >= NG:
                lw, gi = g // NG, g % NG
                src = self.wbf[lw].ap()[gi * 128:(gi + 1) * 128, :]
                rd = [('wbf', lw, gi)]
            else:
                src = self.wall[g]
                rd = []
            dst = self.wb[sl][:]
            self.P.add('pool', lambda e, dst=dst, src=src: e.dma_start(out=dst, in_=src),
                       reads=rd, writes=[('wb', sl)], dma_key=('w', sl))
        return n % nb

    def convert_some(self, lw, n):
        if self.dry or WGATHER or lw not in self.wbf:
            return
        for _ in range(n):
            gi = self.cv_next[lw]
            if gi >= NG:
                return
            self.cv_next[lw] = gi + 1
            src = self.wall[lw * NG + gi]
            dst = self.wbf[lw].ap()[gi * 128:(gi + 1) * 128, :]
            self.P.add('pool', lambda e, dst=dst, src=src: e.dma_start(out=dst, in_=src),
                       writes=[('wbf', lw, gi)], dma_key=self.dk('cv', 8))

    def wk(self, s, kc):
        return self.wb[s][:, kc * 128:(kc + 1) * 128]

    def mm(self, out, lhsT, rhs, start, stop, reads, writes):
        self.P.add('pe', lambda e: e.matmul(out, lhsT=lhsT, rhs=rhs, start=start, stop=stop),
                   reads=reads, writes=writes)

    def act(self, out, in_, func, reads, writes, bias=None, scale=None):
        kw = {}
        if bias is not None:
            kw['bias'] = bias
        if scale is not None:
            kw['scale'] = scale
        self.P.add('act', lambda e: e.activation(out=out, in_=in_, func=func, **kw), reads=reads, writes=writes)

    def dve(self, fn, reads, writes):
        self.P.add('dve', fn, reads=reads, writes=writes)

    def proj_block(self, s, rhs_chunks, rhs_tokens, nk=KC, bank=None, start=True, stop=True, kbase=0):
        if bank is None:
            bank = self.bank()
        for kc in range(nk):
            self.mm(self.psa(bank), self.wk(s, kc), rhs_chunks[kbase + kc],
                    start and kc == 0, stop and kc == nk - 1,
                    reads=[('wb', s), rhs_tokens[kbase + kc]], writes=[('ps', bank)])
        return bank

    def build(self):
        nc = self.nc
        with contextlib.ExitStack() as st:
            sb = lambda name, shape, dtp: st.enter_context(nc.sbuf_tensor(name, shape, dtp))
            self.wb = [sb("wb%d" % i, [128, 2048], BF16) for i in range(6)]
            self.xT = sb("xT", [128, KC * T], F32)
            self.hT = sb("hT", [128, KC * T], BF16)
            self.U = sb("U", [128, 2 * KC * T], BF16)
            self.BIG = sb("BIG", [128, FC * T], BF16)
            self.tmpf = [sb("tf%d" % i, [128, 520], F32) for i in range(8)]
            self.tmpb = [sb("tb%d" % i, [128, 520], BF16) for i in range(12)]
            self.rc = sb("rc", [128, T], F32)
            self.rs = sb("rs", [128, T], F32)
            self.prm = sb("prm_sb", [128, self.nl * NPRM], F32)
            self.rmat = sb("rmat_sb", [128, 128], BF16)
            self.o128 = sb("o128", [128, 128], BF16)
            self.o2048 = sb("o2048", [128, 128], BF16)
            self.obf = sb("obf", [128, 128], BF16)
            self.maskt = sb("maskt", [128, 2], F32)
            self.hedge = sb("hedge", [128, KC * 10], BF16)
            self.halo_st = sb("halo_st", [128, 32], BF16)
            self.halo_out = sb("halo_out", [128, 32], BF16)
            self.epsb = sb("epsb", [128, 1], F32)
            self.rstd_t = sb("rstd_t", [128, T], F32)
            self.uh = sb("uh", [128, KC * 10], F32)
            self.ps = [st.enter_context(nc.psum_tensor("ps%d" % i, [128, 512], F32)) for i in range(8)]
            self.Uf = self.U[:].bitcast(F32)
            P = self.P
            P.add('sp', lambda e: e.dma_start(out=self.prm[:], in_=self.prm_d), writes=['prm'], dma_key='c0')
            P.add('sp', lambda e: e.dma_start(out=self.rmat[:], in_=self.rmat_d), writes=['rmat'], dma_key='c1')
            P.add('sp', lambda e: e.dma_start(out=self.maskt[:], in_=self.mask_d), writes=['mask'], dma_key='c2')
            P.add('dve', lambda e: e.memset(self.o128[:], 1.0 / 128.0), writes=['o128'])
            P.add('dve', lambda e: e.memset(self.o2048[:], 1.0 / 2048.0), writes=['o2048'])
            P.add('dve', lambda e: e.memset(self.obf[:], 1.0), writes=['obf'])
            P.add('dve', lambda e: e.memset(self.hedge[:], 0.0), writes=['hedge'])
            P.add('dve', lambda e: e.memset(self.epsb[:], EPS), writes=['epsb'])
            if WGATHER:
                allc = [list(range(8))]
                for gi in range(self.nl * NGS):
                    r0, r1 = gi * 128, (gi + 1) * 128
                    src = self.wsh.ap()[r0:r1, :]
                    dst = self.wbn.ap()[r0:r1, :]
                    P.add('sp', lambda e, dst=dst, src=src: e.dma_start(out=dst, in_=src),
                          writes=[('wbn', gi)], dma_key=self.dk('wbn', 4))
                for pi in range(self.nl * NPART):
                    l = pi // NPART
                    r0, r1 = pi * 7 * 128, (pi + 1) * 7 * 128
                    a = self.wbn.ap()[r0:r1, :]
                    b = self.wg[pi].ap()
                    P.add('pool', lambda e, a=a, b=b: e.collective_compute("AllGather", ALU.bypass, replica_groups=allc,
                                                                           ins=[a], outs=[b]),
                          reads=[('wbn', pi * 7 + i) for i in range(7)], writes=[('wg', pi)],
                          dma_key='cc', dma_inc=1)
            for l in range(self.nl):
                xa = self.xin if l == 0 else self.xs
                xa_name = 'xin' if l == 0 else 'xs'
                xc = self.out if l == self.nl - 1 else self.xs
                xc_name = 'out' if l == self.nl - 1 else 'xs'
                self.phaseA(l, xa, xa_name)
                self.exchange(l)
                for t in range(NT):
                    self.phaseB(l, t, xa, xa_name)
                    self.phaseC(l, t, xc, xc_name, last=(l == self.nl - 1))
            if self.dry:
                return self.wrec
            P.emit(nc, final_waits=self.fin_ops)
        return nc

    def pcol(self, l, c, n=1):
        return self.prm[:, l * NPRM + c: l * NPRM + c + n]

    def norm_stats(self, src_chunks, src_tokens, n_chunks, omat, omat_tok):
        accb = 7
        for j in range(n_chunks):
            q = self.tb()
            self.act(self.tba(q), src_chunks[j], AF.Square, reads=[src_tokens[j]], writes=[('tb', q)])
            self.mm(self.psa(accb), omat, self.tba(q), j == 0, j == n_chunks - 1,
                    reads=[omat_tok, ('tb', q)], writes=[('ps', accb)])
        return self.rstd_from(accb, dedicated=True)

    def rstd_from(self, bank, dedicated=False):
        q = self.tf()
        qa = self.tfa(q)
        self.act(qa, self.psa(bank), AF.Sqrt, reads=[('ps', bank), 'epsb'], writes=[('tf', q)],
                 bias=self.epsb[:], scale=1.0)
        if dedicated:
            ra, rtok = self.rstd_t[:], 'rstd_t'
        else:
            r = self.tf()
            ra, rtok = self.tfa(r), ('tf', r)
        self.dve(lambda e: e.reciprocal(out=ra, in_=qa), reads=[('tf', q)], writes=[rtok])
        return ra, rtok

    def load_x_and_norm(self, l, t, xsrc, xname, gcol):
        P = self.P
        self.load_xT(xsrc, xname, t)
        self.norm_from_xT(l, gcol)

    def load_xT(self, xsrc, xname, t):
        for k in range(KC):
            src = xsrc[t][:, k * T:(k + 1) * T]
            dst = self.xT[:, k * T:(k + 1) * T]
            self.P.add('sp', lambda e, dst=dst, src=src: e.dma_start(out=dst, in_=src), reads=[(xname, t, k)],
                       writes=[('xT', k)], dma_key=self.dk('xl', 8))

    def norm_from_xT(self, l, gcol):
        xch = [self.xT[:, k * T:(k + 1) * T] for k in range(KC)]
        xtok = [('xT', k) for k in range(KC)]
        rst, rtok = self.norm_stats(xch, xtok, KC, self.o2048[:], 'o2048')
        for k in range(KC):
            o = self.hT[:, k * T:(k + 1) * T]
            i0 = xch[k]
            g = self.pcol(l, gcol + k)
            self.dve(lambda e, o=o, i0=i0, g=g: e.scalar_tensor_tensor(out=o, in0=i0, scalar=g, in1=rst,
                                                                      op0=ALU.mult, op1=ALU.mult),
                     reads=[('xT', k), rtok, 'prm'], writes=[('hT', k)])

    def qk_stages(self, l, bank, gcol, dst, dst_tok, after=None):
        raw = self.psa(bank)
        stt = {}

        def s1():
            q = self.tb()
            stt['q'] = q
            self.act(self.tba(q), raw, AF.Square, reads=[('ps', bank)], writes=[('tb', q)])

        def s2():
            q = stt['q']
            b2 = self.bank()
            self.mm(self.psa(b2), self.o128[:], self.tba(q), True, True, reads=['o128', ('tb', q)], writes=[('ps', b2)])
            rst, rtok = self.rstd_from(b2)
            xn = self.tb()
            stt['xn'] = xn
            xna = self.tba(xn)
            g = self.pcol(l, gcol)
            self.dve(lambda e: e.scalar_tensor_tensor(out=xna, in0=raw, scalar=g, in1=rst, op0=ALU.mult, op1=ALU.mult),
                     reads=[('ps', bank), rtok, 'prm'], writes=[('tb', xn)])

        def s3():
            xn = stt['xn']
            xna = self.tba(xn)
            b3 = self.bank()
            self.mm(self.psa(b3), self.rmat[:], xna, True, True, reads=['rmat', ('tb', xn)], writes=[('ps', b3)])
            t1 = self.tf()
            t1a = self.tfa(t1)
            rc = self.rc[:]
            rs = self.rs[:]
            self.P.add('pool', lambda e: e.tensor_tensor(out=t1a, in0=xna, in1=rc, op=ALU.mult),
                       reads=[('tb', xn), 'rc'], writes=[('tf', t1)])
            t2 = self.tf()
            t2a = self.tfa(t2)
            rx = self.psa(b3)
            self.dve(lambda e: e.tensor_tensor(out=t2a, in0=rx, in1=rs, op=ALU.mult),
                     reads=[('ps', b3), 'rs'], writes=[('tf', t2)])
            self.dve(lambda e: e.tensor_tensor(out=dst, in0=t1a, in1=t2a, op=ALU.add),
                     reads=[('tf', t1), ('tf', t2)], writes=dst_tok)
            if after is not None:
                after()
        return [s1, s2, s3]

    def pipeline(self, mains):
        pend = []

        def advance():
            for p in list(pend):
                p[0][p[1]]()
                p[1] += 1
                if p[1] == len(p[0]):
                    pend.remove(p)
        for m in mains:
            pend.append([m(), 0])
            advance()
        while pend:
            advance()

    def load_rope(self, t):
        c = self.ropec_d[:, t * T:(t + 1) * T]
        s = self.ropes_d[:, t * T:(t + 1) * T]
        self.P.add('sp', lambda e: e.dma_start(out=self.rc[:], in_=c), writes=['rc'], dma_key='rpc')
        self.P.add('sp', lambda e: e.dma_start(out=self.rs[:], in_=s), writes=['rs'], dma_key='rps')

    def phaseA(self, l, xa, xa_name):
        P = self.P
        hch = [self.hT[:, k * T:(k + 1) * T] for k in range(KC)]
        htok = [('hT', k) for k in range(KC)]
        for t in range(NT):
            self.load_rope(t)
            self.load_x_and_norm(l, t, xa, xa_name, 0)
            ho = bass.AP(self.hedge, 1 + 2 * t, [[KC * 10, 128], [10, KC], [1, 2]])
            hi = bass.AP(self.hT, 0, [[KC * T, 128], [T, KC], [T - 1, 2]])
            P.add('act', lambda e, ho=ho, hi=hi: e.copy(out=ho, in_=hi), reads=htok, writes=['hedge'])
            def kmain(h, t=t):
                def m():
                    s_ = self.wload(l * NG + h)
                    bank = self.proj_block(s_, hch, htok)
                    kq = self.tb()

                    def store():
                        dst = self.kin.ap()[h * 128:(h + 1) * 128, t * T:(t + 1) * T]
                        src = self.tba(kq)
                        P.add('sp', lambda e, dst=dst, src=src: e.dma_start(out=dst, in_=src), reads=[('tb', kq)],
                              writes=[('kin', h, t)], dma_key=self.dk('kst', 4))
                    return self.qk_stages(l, bank, 97, self.tba(kq), [('tb', kq)], after=store)
                return m
            self.pipeline([kmain(h) for h in range(NKV)])
            vbanks = [self.bank() for _ in range(4)]
            for vb in range(4):
                s = self.wload(l * NG + 4 + vb)
                for tbk in range(4):
                    for kc in range(KC):
                        self.mm(self.ps[vbanks[tbk]][:, vb * 128:(vb + 1) * 128],
                                self.hT[:, kc * T + tbk * 128: kc * T + (tbk + 1) * 128], self.wk(s, kc),
                                kc == 0, kc == KC - 1,
                                reads=[('wb', s), ('hT', kc)], writes=[('ps', vbanks[tbk])])
            for tbk in range(4):
                vq = self.tb()
                self.act(self.tba(vq), self.psa(vbanks[tbk]), AF.Copy, reads=[('ps', vbanks[tbk])], writes=[('tb', vq)])
                cl = t * 4 + tbk
                dst = bass.AP(self.vin, cl * 128, [[TOK, 128], [128 * TOK, NKV], [1, 128]])
                src = bass.AP(self.tmpb[vq], 0, [[520, 128], [128, NKV], [1, 128]])
                P.add('sp', lambda e, dst=dst, src=src: e.dma_start(out=dst, in_=src), reads=[('tb', vq)],
                      writes=[('vin', tbk, t)], dma_key=self.dk('vst', 4))
        for i, col in enumerate((1, 8)):
            hs = bass.AP(self.hedge, col, [[KC * 10, 128], [10, KC]])
            ho_ = self.halo_out[:, i * 16:(i + 1) * 16]
            P.add('act', lambda e, ho_=ho_, hs=hs: e.copy(out=ho_, in_=hs), reads=['hedge'], writes=[('halo_out', i)])
            dst = self.hin.ap()[i * 128:(i + 1) * 128, :]
            P.add('sp', lambda e, dst=dst, ho_=ho_: e.dma_start(out=dst, in_=ho_),
                  reads=[('halo_out', i)], writes=[('hin', i)], dma_key=('hst', i))

    def exchange(self, l):
        P = self.P
        rg = [[0, 1], [2, 3], [4, 5], [6, 7]]
        intok = {'k': [('kin', h, t) for h in range(NKV) for t in range(NT)],
                 'v': [('vin', tbk, t) for tbk in range(4) for t in range(NT)],
                 'h': [('hin', 0), ('hin', 1)]}
        for name, a, b in (('k', self.kin, self.kout), ('v', self.vin, self.vout), ('h', self.hin, self.hout)):
            P.add('pool', lambda e, a=a, b=b: e.collective_compute("AllGather", ALU.bypass, replica_groups=rg,
                                                                   ins=[a.ap()], outs=[b.ap()]),
                  reads=intok[name], writes=[name + 'out'], dma_key='cc', dma_inc=1)
        for i, r0 in enumerate((128, 256)):
            src = self.hout.ap()[r0:r0 + 128, :]
            dst = self.halo_st[:, i * 16:(i + 1) * 16]
            P.add('sp', lambda e, dst=dst, src=src: e.dma_start(out=dst, in_=src), reads=['hout'],
                  writes=[('halo_st', i)], dma_key=('hld', i))
        for i, col in enumerate((0, 9)):
            o = bass.AP(self.hedge, col, [[KC * 10, 128], [10, KC]])
            i0 = self.halo_st[:, i * 16:(i + 1) * 16]
            m = self.maskt[:, i:i + 1]
            self.dve(lambda e, o=o, i0=i0, m=m: e.tensor_scalar(out=o, in0=i0, scalar1=m, scalar2=None, op0=ALU.mult),
                     reads=[('halo_st', i), 'mask', 'hedge'], writes=['hedge'])

    def Bt(self, lo, n):
        return [('B', i) for i in range(lo, lo + n)]

    def phaseB(self, l, t, xa, xa_name):
        P = self.P
        g0 = l * NG
        hch = [self.hT[:, k * T:(k + 1) * T] for k in range(KC)]
        htok = [('hT', k) for k in range(KC)]
        self.load_rope(t)
        self.load_x_and_norm(l, t, xa, xa_name, 0)
        dbg = (l == 0 and t == 0)
        if dbg:
            self.dump('hT', self.hT[:], htok, KC * T, BF16)
        cbch = [self.U[:, k * T:(k + 1) * T] for k in range(KC)]
        cbtok = [('U', k) for k in range(KC)]
        for j in range(KC):
            s = self.wload(g0 + 8 + 3 * j)
            bank = self.proj_block(s, hch, htok)
            if t == 0:
                hb = self.bank()
                for kc in range(KC):
                    self.mm(self.ps[hb][:, 0:10], self.wk(s, kc), self.hedge[:, kc * 10:(kc + 1) * 10],
                            kc == 0, kc == KC - 1, reads=[('wb', s), 'hedge'], writes=[('ps', hb)])
                ch = self.tf()
                self.act(self.tfa(ch, 10), self.ps[hb][:, 0:10], AF.Copy, reads=[('ps', hb)], writes=[('tf', ch)])
            ce = self.tf()
            self.act(self.tfa(ce), self.psa(bank), AF.Copy, reads=[('ps', bank)], writes=[('tf', ce)])
            s = self.wload(g0 + 8 + 3 * j + 1)
            bank = self.proj_block(s, hch, htok)
            if t == 0:
                hb = self.bank()
                for kc in range(KC):
                    self.mm(self.ps[hb][:, 0:10], self.wk(s, kc), self.hedge[:, kc * 10:(kc + 1) * 10],
                            kc == 0, kc == KC - 1, reads=[('wb', s), 'hedge'], writes=[('ps', hb)])
                uho = self.uh[:, j * 10:(j + 1) * 10]
                cha = self.tfa(ch, 10)
                hsrc = self.ps[hb][:, 0:10]
                self.dve(lambda e, uho=uho, cha=cha, hsrc=hsrc: e.tensor_tensor(out=uho, in0=hsrc, in1=cha, op=ALU.mult),
                         reads=[('ps', hb), ('tf', ch)], writes=[('uh', j)])
            ue = self.tf()
            uo = self.tfa(ue, T, 1)
            ci = self.tfa(ce)
            pin = self.psa(bank)
            self.dve(lambda e, uo=uo, ci=ci, pin=pin: e.tensor_tensor(out=uo, in0=pin, in1=ci, op=ALU.mult),
                     reads=[('ps', bank), ('tf', ce)], writes=[('tf', ue)])
            ueo = bass.AP(self.tmpf[ue], 0, [[520, 128], [T + 1, 2]])
            uhi = bass.AP(self.uh, j * 10 + 2 * t, [[KC * 10, 128], [3, 2]])
            P.add('act', lambda e, ueo=ueo, uhi=uhi: e.copy(out=ueo, in_=uhi), reads=[('uh', j)], writes=[('tf', ue)])
            c1 = self.tf()
            self.act(self.tfa(c1), self.tfa(ue, T, 1), AF.Copy, reads=[('tf', ue), 'prm'], writes=[('tf', c1)],
                     scale=self.pcol(l, 48 + 16 + j))
            c2 = self.tf()
            a0 = self.tfa(ue, T, 0)
            a2 = self.tfa(ue, T, 2)
            w0 = self.pcol(l, 48 + j)
            w2 = self.pcol(l, 48 + 32 + j)
            c1a = self.tfa(c1)
            c2a = self.tfa(c2)
            self.dve(lambda e, a0=a0, w0=w0, c1a=c1a, c2a=c2a: e.scalar_tensor_tensor(
                out=c2a, in0=a0, scalar=w0, in1=c1a, op0=ALU.mult, op1=ALU.add),
                reads=[('tf', ue), ('tf', c1), 'prm'], writes=[('tf', c2)])
            c3 = self.tf()
            c3a = self.tfa(c3)
            self.dve(lambda e, a2=a2, w2=w2, c2a=c2a, c3a=c3a: e.scalar_tensor_tensor(
                out=c3a, in0=a2, scalar=w2, in1=c2a, op0=ALU.mult, op1=ALU.add),
                reads=[('tf', ue), ('tf', c2), 'prm'], writes=[('tf', c3)])
            s = self.wload(g0 + 8 + 3 * j + 2)
            bank = self.proj_block(s, hch, htok)
            pb = self.psa(bank)
            o = cbch[j]
            self.dve(lambda e, o=o, pb=pb, c3a=c3a: e.tensor_tensor(out=o, in0=pb, in1=c3a, op=ALU.mult),
                     reads=[('ps', bank), ('tf', c3)], writes=[cbtok[j]])
        QB = 32
        qch = [self.BIG[:, (QB + h) * T:(QB + h + 1) * T] for h in range(NQ)]
        qtok = [('B', QB + h) for h in range(NQ)]
        def qmain(h):
            def m():
                s_ = self.wload(g0 + 56 + h)
                bank = self.proj_block(s_, hch, htok)
                return self.qk_stages(l, bank, 96, qch[h], [qtok[h]])
            return m
        self.pipeline([qmain(h) for h in range(NQ)])
        if dbg:
            self.dump('cbT', self.U[:, 0:KC * T], cbtok, KC * T, BF16)
            self.dump('qT', self.BIG[:, QB * T:(QB + NQ) * T], qtok, KC * T, BF16)
        self.attention(l, t, qch, qtok)
        if dbg:
            self.dump('atT', self.U[:, KC * T:2 * KC * T], [('U', KC + k) for k in range(KC)], KC * T, BF16)
        atch = [self.U[:, (KC + k) * T:(KC + k + 1) * T] for k in range(KC)]
        attok = [('U', KC + k) for k in range(KC)]
        mch = qch
        mtok = qtok
        for j in range(KC):
            s = self.wload(g0 + 72 + 4 * j + 2)
            bga = self.proj_block(s, hch, htok)
            s = self.wload(g0 + 72 + 4 * j + 3)
            bgb = self.proj_block(s, hch, htok)
            s = self.wload(g0 + 72 + 4 * j)
            bya = self.proj_block(s, cbch, cbtok)
            s = self.wload(g0 + 72 + 4 * j + 1)
            byb = self.proj_block(s, atch, attok)
            sa = self.tf()
            self.act(self.tfa(sa), self.psa(bga), AF.Sigmoid, reads=[('ps', bga), 'prm'], writes=[('tf', sa)],
                     bias=self.pcol(l, 16 + j))
            sbb = self.tf()
            self.act(self.tfa(sbb), self.psa(bgb), AF.Sigmoid, reads=[('ps', bgb), 'prm'], writes=[('tf', sbb)],
                     bias=self.pcol(l, 32 + j))
            m1 = self.tf()
            m1a, saa, sba = self.tfa(m1), self.tfa(sa), self.tfa(sbb)
            pya, pyb = self.psa(bya), self.psa(byb)
            self.dve(lambda e, m1a=m1a, pya=pya, saa=saa: e.tensor_tensor(out=m1a, in0=pya, in1=saa, op=ALU.mult),
                     reads=[('ps', bya), ('tf', sa)], writes=[('tf', m1)])
            m2 = self.tf()
            m2a = self.tfa(m2)
            self.dve(lambda e, m2a=m2a, pyb=pyb, sba=sba: e.tensor_tensor(out=m2a, in0=pyb, in1=sba, op=ALU.mult),
                     reads=[('ps', byb), ('tf', sbb)], writes=[('tf', m2)])
            o = mch[j]
            P.add('pool', lambda e, o=o, m1a=m1a, m2a=m2a: e.tensor_tensor(out=o, in0=m1a, in1=m2a, op=ALU.add),
                  reads=[('tf', m1), ('tf', m2)], writes=[mtok[j]])
        if dbg:
            self.dump('mT', self.BIG[:, QB * T:(QB + NQ) * T], mtok, KC * T, BF16)
        self.proj_norm_residual(l, g0 + 136, 1, mch, mtok, 98, None, 'xmid', t)

    def proj_norm_residual(self, l, gbase, gper, rch, rtok, gcol, xdst, xdst_name, t):
        P = self.P
        fch = [self.Uf[:, j * T:(j + 1) * T] for j in range(KC)]
        ftok = [[('U', 2 * j), ('U', 2 * j + 1)] for j in range(KC)]
        accb = 7
        pend = None

        def ones(jq):
            jj, q = jq
            self.mm(self.psa(accb), self.o2048[:], self.tba(q), jj == 0, jj == KC - 1,
                    reads=['o2048', ('tb', q)], writes=[('ps', accb)])
        for j in range(KC):
            bank = self.bank()
            for gi in range(gper):
                s = self.wload(gbase + j * gper + gi)
                self.proj_block(s, rch, rtok, bank=bank, start=(gi == 0), stop=(gi == gper - 1), kbase=gi * KC)
            q = self.tb()
            self.act(self.tba(q), self.psa(bank), AF.Square, reads=[('ps', bank)], writes=[('tb', q)])
            self.act(fch[j], self.psa(bank), AF.Copy, reads=[('ps', bank)], writes=ftok[j])
            if pend is not None:
                ones(pend)
            pend = (j, q)
        ones(pend)
        rst, rtok = self.rstd_from(accb, dedicated=True)
        for j in range(KC):
            tm = self.tf()
            tma = self.tfa(tm)
            g = self.pcol(l, gcol + j)
            fj = fch[j]
            self.dve(lambda e, tma=tma, fj=fj, g=g: e.scalar_tensor_tensor(out=tma, in0=fj, scalar=g, in1=rst,
                                                                          op0=ALU.mult, op1=ALU.mult),
                     reads=ftok[j] + [rtok, 'prm'], writes=[('tf', tm)])
            xj = self.xT[:, j * T:(j + 1) * T]
            P.add('pool', lambda e, xj=xj, tma=tma: e.tensor_tensor(out=xj, in0=xj, in1=tma, op=ALU.add),
                  reads=[('xT', j), ('tf', tm)], writes=[('xT', j)])
            if xdst is None:
                continue
            dst = xdst[t][:, j * T:(j + 1) * T]
            op = P.add('sp', lambda e, dst=dst, xj=xj: e.dma_start(out=dst, in_=xj), reads=[('xT', j)],
                       writes=[(xdst_name, t, j)], dma_key=self.dk('xst', 8))
            if xdst_name == 'out':
                self.fin_ops.append(op)
        if l == 0 and t == 0 and xdst_name == 'xmid':
            self.dump('xm', self.xT[:], [('xT', k) for k in range(KC)], KC * T, F32)

    def attention(self, l, t, qch, qtok):
        P = self.P
        def kslot(i):
            return self.BIG[:, i * 8 * T:(i + 1) * 8 * T]

        def vslot(i):
            return self.BIG[:, (16 + i * 8) * T:(16 + (i + 1) * 8) * T]
        accf = self.BIG[:, 52 * T:56 * T].bitcast(F32)
        for kvh in range(NKV):
            ks = kvh % 2
            ktok = self.Bt(ks * 8, 8)
            vtok = self.Bt(16 + ks * 8, 8)
            for r in range(2):
                src = self.kout.ap()[r * 512 + kvh * 128: r * 512 + (kvh + 1) * 128, :]
                dst = self.BIG[:, (ks * 8) * T + r * TOK:(ks * 8) * T + (r + 1) * TOK]
                P.add('sp', lambda e, dst=dst, src=src: e.dma_start(out=dst, in_=src), reads=['kout'],
                      writes=self.Bt(ks * 8 + 4 * r, 4), dma_key=('kl', ks, r))
            for r in range(2):
                src = self.vout.ap()[r * 512 + kvh * 128: r * 512 + (kvh + 1) * 128, :]
                dst = self.BIG[:, (16 + ks * 8) * T + r * TOK:(16 + ks * 8) * T + (r + 1) * TOK]
                P.add('sp', lambda e, dst=dst, src=src: e.dma_start(out=dst, in_=src), reads=['vout'],
                      writes=self.Bt(16 + ks * 8 + 4 * r, 4), dma_key=('vl', ks, r))
            if l == 0 and t == 0 and kvh == 0:
                self.dump('K0', self.BIG[:, 0:8 * T], self.Bt(0, 8), 8 * T, BF16)
                self.dump('V0', self.BIG[:, 16 * T:24 * T], self.Bt(16, 8), 8 * T, BF16)
            for hq in range(4):
                h = kvh * 4 + hq
                self.convert_some(l + 1, 5 if h % 8 < 3 else 4)
                ob = 6 + (h % 2)
                ai = h % 2
                acc = accf[:, ai * T:(ai + 1) * T]
                acctok = self.Bt(52 + 2 * ai, 2)
                NCH = 32
                sbanks = {}
                pts = {}

                dbk = 4 + (h % 2)

                def S(c):
                    b = self.bank_rr % 4
                    self.bank_rr += 1
                    sbanks[c] = b
                    self.mm(self.psa(b), self.BIG[:, ks * 8 * T + c * 128: ks * 8 * T + (c + 1) * 128], qch[h],
                            True, True, reads=self.Bt(ks * 8 + 4 * (c // 16), 4) + [qtok[h]], writes=[('ps', b)])

                def E(c):
                    b = sbanks.pop(c)
                    pi = c % 8
                    pts[c] = pi
                    pt = self.BIG[:, (48 + pi) * T:(49 + pi) * T]
                    self.act(pt, self.psa(b), AF.Exp, reads=[('ps', b)], writes=[('B', 48 + pi)], scale=QSCALE)

                def PV(c):
                    pi = pts[c]
                    pt = self.BIG[:, (48 + pi) * T:(49 + pi) * T]
                    vv = self.BIG[:, (16 + ks * 8) * T + c * 128:(16 + ks * 8) * T + (c + 1) * 128]
                    self.mm(self.psa(ob), vv, pt, c == 0, c == NCH - 1,
                            reads=self.Bt(16 + ks * 8 + 4 * (c // 16), 4) + [('B', 48 + pi)], writes=[('ps', ob)])

                gsum = {}

                def GS(g):
                    p = [self.BIG[:, (48 + pts[4 * g + i]) * T:(49 + pts[4 * g + i]) * T] for i in range(4)]
                    pk = [('B', 48 + pts[4 * g + i]) for i in range(4)]
                    a = self.tb()
                    aa = self.tba(a)
                    self.dve(lambda e, aa=aa, p=p: e.tensor_tensor(out=aa, in0=p[0], in1=p[1], op=ALU.add),
                             reads=pk[0:2], writes=[('tb', a)])
                    b_ = self.tb()
                    ba = self.tba(b_)
                    self.dve(lambda e, ba=ba, p=p: e.tensor_tensor(out=ba, in0=p[2], in1=p[3], op=ALU.add),
                             reads=pk[2:4], writes=[('tb', b_)])
                    sg = self.tb()
                    sga = self.tba(sg)
                    self.dve(lambda e, sga=sga, aa=aa, ba=ba: e.tensor_tensor(out=sga, in0=aa, in1=ba, op=ALU.add),
                             reads=[('tb', a), ('tb', b_)], writes=[('tb', sg)])
                    gsum[g] = sg

                def DEN(g):
                    sg = gsum.pop(g)
                    self.mm(self.psa(dbk), self.obf[:], self.tba(sg), g == 0, g == NCH // 4 - 1,
                            reads=['obf', ('tb', sg)], writes=[('ps', dbk)])
                S(0)
                S(1)
                for c in range(NCH):
                    E(c)
                    if c + 2 < NCH:
                        S(c + 2)
                    PV(c)
                    if c % 4 == 3:
                        GS(c // 4)
                    if c % 4 == 1 and c > 4:
                        DEN(c // 4 - 1)
                DEN(NCH // 4 - 1)
                rd = self.tf()
                rda = self.tfa(rd)
                den = self.psa(dbk)
                self.dve(lambda e, rda=rda, den=den: e.reciprocal(out=rda, in_=den), reads=[('ps', dbk)], writes=[('tf', rd)])
                o = self.U[:, (KC + h) * T:(KC + h + 1) * T]
                po = self.psa(ob)
                self.dve(lambda e, o=o, po=po, rda=rda: e.tensor_tensor(out=o, in0=po, in1=rda, op=ALU.mult),
                         reads=[('ps', ob), ('tf', rd)], writes=[('U', KC + h)])

    def phaseC(self, l, t, xc, xc_name, last):
        P = self.P
        g0 = l * NG
        self.norm_from_xT(l, 114)
        hch = [self.hT[:, k * T:(k + 1) * T] for k in range(KC)]
        htok = [('hT', k) for k in range(KC)]
        ach = [self.BIG[:, f * T:(f + 1) * T] for f in range(FC)]
        atok = [('B', f) for f in range(FC)]
        for f in range(FC):
            s = self.wload(g0 + 152 + f)
            bank = self.proj_block(s, hch, htok)
            r = self.tf()
            self.act(self.tfa(r), self.psa(bank), AF.Relu, reads=[('ps', bank)], writes=[('tf', r)])
            ra = self.tfa(r)
            o = ach[f]
            if f % 2 == 0:
                self.dve(lambda e, o=o, ra=ra: e.tensor_tensor(out=o, in0=ra, in1=ra, op=ALU.mult),
                         reads=[('tf', r)], writes=[atok[f]])
            else:
                P.add('pool', lambda e, o=o, ra=ra: e.tensor_tensor(out=o, in0=ra, in1=ra, op=ALU.mult),
                      reads=[('tf', r)], writes=[atok[f]])
        self.proj_norm_residual(l, g0 + 216, 4, ach, atok, 130, xc, xc_name, t)


def _grp(W, cols):
    blk = W[:, cols]
    return blk.reshape(16, 128, 128).transpose(1, 0, 2).reshape(128, 2048)


def _layer_groups(w_in, w_oc, w_oa, w_mg, w_up, w_dn):
    out = np.empty((NG, 128, 2048), np.float32)
    ar = np.arange(128)
    g = 0
    for h in range(4):
        out[g] = _grp(w_in, OFF_K + h * 128 + ar); g += 1
    for vb in range(4):
        out[g] = _grp(w_in, OFF_V + vb * 128 + ar); g += 1
    for j in range(16):
        for base in (OFF_C, OFF_IN, OFF_B):
            out[g] = _grp(w_in, base + j * 128 + ar); g += 1
    for h in range(16):
        out[g] = _grp(w_in, OFF_Q + h * 128 + ar); g += 1
    for j in range(16):
        out[g] = _grp(w_oc, j * 128 + ar); g += 1
        out[g] = _grp(w_oa, j * 128 + ar); g += 1
        out[g] = _grp(w_in, OFF_GA + j * 128 + ar); g += 1
        out[g] = _grp(w_in, OFF_GB + j * 128 + ar); g += 1
    for j in range(16):
        out[g] = _grp(w_mg, j * 128 + ar); g += 1
    for f in range(64):
        out[g] = _grp(w_up, f * 128 + ar); g += 1
    for ob in range(16):
        for q in range(4):
            blk = w_dn[q * 2048:(q + 1) * 2048, ob * 128:(ob + 1) * 128]
            out[g] = blk.reshape(16, 128, 128).transpose(1, 0, 2).reshape(128, 2048); g += 1
    assert g == NG
    return out


def _chunk_cols(v):
    return np.ascontiguousarray(v.reshape(16, 128).T)


def _layer_params(l, inp):
    p = np.empty((128, NPRM), np.float32)
    p[:, 0:16] = _chunk_cols(inp['norm_mix_pre'][l])
    p[:, 16:32] = _chunk_cols(inp['gate_bias'][l][:2048])
    p[:, 32:48] = _chunk_cols(inp['gate_bias'][l][2048:])
    for i in range(3):
        p[:, 48 + 16 * i:64 + 16 * i] = _chunk_cols(inp['conv_w'][l][i])
    p[:, 96] = inp['q_norm'][l]
    p[:, 97] = inp['k_norm'][l]
    p[:, 98:114] = _chunk_cols(inp['norm_mix_post'][l])
    p[:, 114:130] = _chunk_cols(inp['norm_mlp_pre'][l])
    p[:, 130:146] = _chunk_cols(inp['norm_mlp_post'][l])
    return p


def _rope_tables(half):
    tok = np.arange(half * TOK, (half + 1) * TOK)
    row = (tok // 64).astype(np.float32)
    col = (tok % 64).astype(np.float32)
    inv = (10000.0 ** (-np.arange(0, 64, 2, dtype=np.float32) / 64.0)).astype(np.float32)
    d = np.arange(128)
    f = d % 32
    pos = np.where((d // 64)[:, None] == 0, row[None, :], col[None, :]).astype(np.float32)
    ang = (pos * inv[f][:, None]).astype(np.float32)
    return np.cos(ang).astype(np.float32), np.sin(ang).astype(np.float32)


def _rmat():
    m = np.zeros((128, 128), np.float32)
    for d in range(128):
        if (d % 64) < 32:
            m[d + 32, d] = -1.0
        else:
            m[d - 32, d] = 1.0
    return m.astype(ml_dtypes.bfloat16)


def _x_to_tiles(xc):
    return np.ascontiguousarray(xc.reshape(NT, T, KC, 128).transpose(0, 3, 2, 1).reshape(NT, 128, KC * T))


def _tiles_to_x(o):
    return o.reshape(NT, 128, KC, T).transpose(0, 3, 2, 1).reshape(TOK, D)


_NC_CACHE = {}


def _get_nc(nlayers):
    if nlayers not in _NC_CACHE:
        wseq = Builder(nlayers).build()
        _NC_CACHE[nlayers] = Builder(nlayers, wseq).build()
    return _NC_CACHE[nlayers]


def _run_layers(x_tiles, layers, inp, fused_nc=None):
    nl = len(layers)
    nc = _get_nc(nl)
    lg = [_layer_groups(inp['w_in'][l], inp['w_out_conv'][l], inp['w_out_attn'][l],
                        inp['w_merge'][l], inp['w_up'][l], inp['w_down'][l]) for l in layers]
    if not WGATHER:
        wall = np.concatenate(lg, axis=0)
    prm = np.concatenate([_layer_params(l, inp) for l in layers], axis=1)
    rm = _rmat()
    in_maps = []
    for c in range(8):
        half = c % 2
        rc, rs = _rope_tables(half)
        mask = np.zeros((128, 2), np.float32)
        mask[:, 0] = 1.0 if half == 1 else 0.0
        mask[:, 1] = 1.0 if half == 0 else 0.0
        m = {"xin": x_tiles[c], "prm": prm, "ropec": rc, "ropes": rs, "rmat": rm, "mask": mask}
        if WGATHER:
            m["wsh"] = np.concatenate([g[p * NGP + c * 7: p * NGP + (c + 1) * 7] for g in lg for p in range(NPART)],
                                      axis=0).reshape(nl * NGS * 128, 2048)
        else:
            m["wall"] = wall
        in_maps.append(m)
    res = run_bass_kernel_spmd(nc, in_maps, core_ids=list(range(8)))
    if DEBUG:
        global _DBG
        _DBG = res.results
    return [np.asarray(res.results[c]["out"]) for c in range(8)]


FUSED = True


def kernel(**inputs):
    inp = {k: np.asarray(v) for k, v in inputs.items()}
    x = inp['x']
    tiles = []
    for c in range(8):
        b, half = c // 2, c % 2
        tiles.append(_x_to_tiles(x[b, half * TOK:(half + 1) * TOK, :]))
    if FUSED:
        tiles = _run_layers(tiles, list(range(NL)), inp)
    else:
        for l in range(NL):
            tiles = _run_layers(tiles, [l], inp)
    out = np.empty((4, 4096, D), np.float32)
    for c in range(8):
        b, half = c // 2, c % 2
        out[b, half * TOK:(half + 1) * TOK, :] = _tiles_to_x(tiles[c])
    return out
```

```python
import contextlib
import numpy as np
import ml_dtypes
import concourse.bass as bass
import concourse.mybir as mybir
from concourse.bass_utils import run_bass_kernel_spmd

F32 = mybir.dt.float32
BF16 = mybir.dt.bfloat16
AF = mybir.ActivationFunctionType
ALU = mybir.AluOpType

ENGS = ['pe', 'act', 'dve', 'pool', 'sp']
EIDX = {e: i for i, e in enumerate(ENGS)}

D = 2048
KC = 16
T = 512
NT = 4
TOK = 2048
HD = 128
NQ = 16
NKV = 4
DFF = 8192
FC = 64
NL = 4
NG = 280
NGS = NG // 8
NPART = 5
NGP = NG // NPART
DEBUG = False
PRECONV = False
DBG_ONLY = None
WGATHER = False
NPRM = 146
EPS = 1e-6
QSCALE = 1.0 / float(np.sqrt(128.0))
OFF_B, OFF_C, OFF_IN, OFF_Q, OFF_K, OFF_V, OFF_GA, OFF_GB = 0, 2048, 4096, 6144, 8192, 8704, 9216, 11264


class Op:
    __slots__ = ('eng', 'fn', 'idx', 'deps', 'dma_key', 'dma_val', 'dma_inc', 'signal',
                 'waits', 'vc', 'rank')


class Prog:
    def __init__(self):
        self.ops = {e: [] for e in ENGS}
        self.order = []
        self.last_w = {}
        self.readers = {}
        self.dma_count = {}
        self.last_dma = {}

    def add(self, eng, fn, reads=(), writes=(), dma_key=None, dma_inc=16):
        op = Op()
        op.eng = eng
        op.fn = fn
        op.idx = len(self.ops[eng])
        op.dma_key = dma_key
        op.dma_inc = dma_inc
        op.signal = False
        op.waits = []
        op.rank = None
        deps = {}
        for t in reads:
            w = self.last_w.get(t)
            if w is not None:
                deps[w] = True
        for t in writes:
            w = self.last_w.get(t)
            if w is not None and w not in deps:
                deps[w] = False
            for r in self.readers.get(t, ()):
                if r not in deps:
                    deps[r] = False
        if dma_key is not None:
            prev = self.last_dma.get(dma_key)
            if prev is not None and prev not in deps:
                deps[prev] = False
            self.last_dma[dma_key] = op
        op.deps = deps
        if dma_key is not None:
            self.dma_count[dma_key] = self.dma_count.get(dma_key, 0) + dma_inc
            op.dma_val = self.dma_count[dma_key]
        for t in reads:
            self.readers.setdefault(t, []).append(op)
        for t in writes:
            self.last_w[t] = op
            self.readers[t] = []
        self.ops[eng].append(op)
        self.order.append(op)
        return op

    def resolve(self):
        seen = {e: [-1] * 5 for e in ENGS}
        dseen = {e: {} for e in ENGS}
        for op in self.order:
            F = op.eng
            sF = seen[F]
            dF = dseen[F]
            best = {}
            for a, raw in op.deps.items():
                if a.dma_key is not None:
                    if dF.get(a.dma_key, 0) >= a.dma_val:
                        continue
                    op.waits.append(a)
                    dF[a.dma_key] = a.dma_val
                    for i in range(5):
                        if a.vc[i] > sF[i]:
                            sF[i] = a.vc[i]
                    continue
                E = a.eng
                if E == F and (E == 'pe' or not raw):
                    continue
                cur = best.get(E)
                if cur is None or a.idx > cur.idx:
                    best[E] = a
            for E, a in best.items():
                if sF[EIDX[E]] >= a.idx:
                    continue
                a.signal = True
                op.waits.append(a)
                for i in range(5):
                    if a.vc[i] > sF[i]:
                        sF[i] = a.vc[i]
            vc = list(sF)
            if op.dma_key is None:
                vc[EIDX[F]] = op.idx
            op.vc = vc
            op.deps = None

    def emit(self, nc, final_waits=()):
        self.resolve()
        for a in final_waits:
            if a.dma_key is None:
                a.signal = True
        for e in ENGS:
            n = 0
            for op in self.ops[e]:
                if op.dma_key is None and op.signal:
                    n += 1
                    op.rank = n
        sems = {}
        dsems = {}
        with contextlib.ExitStack() as st:
            for e in ENGS:
                sems[e] = st.enter_context(nc.semaphore('s_' + e))
            for i, k in enumerate(self.dma_count):
                dsems[k] = st.enter_context(nc.semaphore('d%d' % i))
            block = st.enter_context(nc.Block())

            def wait(eng, a):
                if a.dma_key is not None:
                    eng.wait_ge(dsems[a.dma_key], a.dma_val)
                else:
                    eng.wait_ge(sems[a.eng], a.rank)

            def run(engname, eng):
                for op in self.ops[engname]:
                    for a in op.waits:
                        wait(eng, a)
                    ins = op.fn(eng)
                    if op.dma_key is not None:
                        ins.then_inc(dsems[op.dma_key], op.dma_inc)
                    elif op.signal:
                        ins.then_inc(sems[engname], 1)
                if engname == 'sp':
                    for a in final_waits:
                        wait(eng, a)

            @block.tensor
            def _(eng):
                run('pe', eng)

            @block.scalar
            def _(eng):
                run('act', eng)

            @block.vector
            def _(eng):
                run('dve', eng)

            @block.gpsimd
            def _(eng):
                run('pool', eng)

            @block.sync
            def _(eng):
                run('sp', eng)


class DummyProg:
    def add(self, *a, **k):
        return None


class Builder:
    def __init__(self, nlayers, wseq=None):
        self.nl = nlayers
        self.wseq = wseq
        self.dry = wseq is None
        self.wrec = []
        self.wissued = 0
        self.dkc = {}
        nc = bass.Bass("TRN2", target_bir_lowering=False)
        self.nc = nc
        self.P = DummyProg() if self.dry else Prog()
        dt = nc.dram_tensor
        self.xin = dt("xin", [NT, 128, KC * T], F32, kind="ExternalInput").ap()
        self.out = dt("out", [NT, 128, KC * T], F32, kind="ExternalOutput").ap()
        if WGATHER:
            self.wsh = dt("wsh", [nlayers * NGS * 128, 2048], F32, kind="ExternalInput")
            self.wbn = dt("wbn", [nlayers * NGS * 128, 2048], F32)
            self.wg = [dt("wg%d" % i, [NGP * 128, 2048], F32) for i in range(nlayers * NPART)]
        else:
            self.wall = dt("wall", [nlayers * NG, 128, 2048], F32, kind="ExternalInput").ap()
            self.wbf = {l: dt("wbf%d" % l, [NG * 128, 2048], BF16) for l in range(1, nlayers)} if PRECONV else {}
            self.cv_next = {l: 0 for l in range(1, nlayers)}
        self.prm_d = dt("prm", [128, nlayers * NPRM], F32, kind="ExternalInput").ap()
        self.ropec_d = dt("ropec", [128, TOK], F32, kind="ExternalInput").ap()
        self.ropes_d = dt("ropes", [128, TOK], F32, kind="ExternalInput").ap()
        self.rmat_d = dt("rmat", [128, 128], BF16, kind="ExternalInput").ap()
        self.mask_d = dt("mask", [128, 2], F32, kind="ExternalInput").ap()
        self.xmid = dt("xmid", [NT, 128, KC * T], F32).ap()
        self.xs = dt("xs", [NT, 128, KC * T], F32).ap()
        self.kin = dt("kin", [NKV * 128, TOK], BF16)
        self.kout = dt("kout", [2 * NKV * 128, TOK], BF16)
        self.vin = dt("vin", [NKV * 128, TOK], BF16)
        self.vout = dt("vout", [2 * NKV * 128, TOK], BF16)
        self.hin = dt("hin", [256, 16], BF16)
        self.hout = dt("hout", [512, 16], BF16)
        self.wctr = 0
        self.bank_rr = 0
        self.tf_rr = 0
        self.tb_rr = 0
        self.fin_ops = []
        self.dbg_names = []

    def dump(self, name, ap, toks, ncols, dtp):
        if not DEBUG or self.dry or (DBG_ONLY is not None and name not in DBG_ONLY):
            return
        d = self.nc.dram_tensor("dbg_" + name, [128, ncols], dtp, kind="ExternalOutput").ap()
        op = self.P.add('sp', lambda e: e.dma_start(out=d, in_=ap), reads=toks, writes=['dbg_' + name],
                        dma_key='dbg_' + name)
        self.fin_ops.append(op)
        self.dbg_names.append("dbg_" + name)

    def bank(self):
        b = self.bank_rr % 7
        self.bank_rr += 1
        return b

    def tf(self):
        i = self.tf_rr % len(self.tmpf)
        self.tf_rr += 1
        return i

    def tb(self):
        i = self.tb_rr % len(self.tmpb)
        self.tb_rr += 1
        return i

    def tfa(self, i, n=T, off=0):
        return self.tmpf[i][:, off:off + n]

    def tba(self, i, n=T, off=0):
        return self.tmpb[i][:, off:off + n]

    def psa(self, b, n=T, off=0):
        return self.ps[b][:, off:off + n]

    def dk(self, cls, n):
        i = self.dkc.get(cls, 0)
        self.dkc[cls] = i + 1
        return (cls, i % n)

    def wload(self, gidx):
        n = self.wctr
        self.wctr += 1
        nb = len(self.wb)
        if self.dry:
            self.wrec.append(gidx)
            return n % nb
        assert self.wseq[n] == gidx
        lim = min(n + nb - 1, len(self.wseq) - 1)
        while self.wissued <= lim:
            i = self.wissued
            self.wissued += 1
            sl = i % nb
            g = self.wseq[i]
            if WGATHER:
                part, gi = g // NGP, g % NGP
                src = self.wg[part].ap()[gi * 128:(gi + 1) * 128, :]
                rd = [('wg', part)]
            elif g >= NG and PRECONV:
                lw, gi = g // NG, g % NG
                src = self.wbf[lw].ap()[gi * 128:(gi + 1) * 128, :]
                rd = [('wbf', lw, gi)]
            else:
                src = self.wall[g]
                rd = []
            dst = self.wb[sl][:]
            self.P.add('pool', lambda e, dst=dst, src=src: e.dma_start(out=dst, in_=src),
                       reads=rd, writes=[('wb', sl)], dma_key=('w', sl))
        return n % nb

    def convert_some(self, lw, n):
        if self.dry or WGATHER or lw not in self.wbf:
            return
        for _ in range(n):
            gi = self.cv_next[lw]
            if gi >= NG:
                return
            self.cv_next[lw] = gi + 1
            src = self.wall[lw * NG + gi]
            dst = self.wbf[lw].ap()[gi * 128:(gi + 1) * 128, :]
            self.P.add('pool', lambda e, dst=dst, src=src: e.dma_start(out=dst, in_=src),
                       writes=[('wbf', lw, gi)], dma_key=self.dk('cv', 8))

    def wk(self, s, kc):
        return self.wb[s][:, kc * 128:(kc + 1) * 128]

    def mm(self, out, lhsT, rhs, start, stop, reads, writes):
        self.P.add('pe', lambda e: e.matmul(out, lhsT=lhsT, rhs=rhs, start=start, stop=stop),
                   reads=reads, writes=writes)

    def act(self, out, in_, func, reads, writes, bias=None, scale=None):
        kw = {}
        if bias is not None:
            kw['bias'] = bias
        if scale is not None:
            kw['scale'] = scale
        self.P.add('act', lambda e: e.activation(out=out, in_=in_, func=func, **kw), reads=reads, writes=writes)

    def dve(self, fn, reads, writes):
        self.P.add('dve', fn, reads=reads, writes=writes)

    def proj_block(self, s, rhs_chunks, rhs_tokens, nk=KC, bank=None, start=True, stop=True, kbase=0):
        if bank is None:
            bank = self.bank()
        for kc in range(nk):
            self.mm(self.psa(bank), self.wk(s, kc), rhs_chunks[kbase + kc],
                    start and kc == 0, stop and kc == nk - 1,
                    reads=[('wb', s), rhs_tokens[kbase + kc]], writes=[('ps', bank)])
        return bank

    def build(self):
        nc = self.nc
        with contextlib.ExitStack() as st:
            sb = lambda name, shape, dtp: st.enter_context(nc.sbuf_tensor(name, shape, dtp))
            self.wb = [sb("wb%d" % i, [128, 2048], BF16) for i in range(6)]
            self.xT = sb("xT", [128, KC * T], F32)
            self.hT = sb("hT", [128, KC * T], BF16)
            self.U = sb("U", [128, 2 * KC * T], BF16)
            self.BIG = sb("BIG", [128, FC * T], BF16)
            self.tmpf = [sb("tf%d" % i, [128, 520], F32) for i in range(8)]
            self.tmpb = [sb("tb%d" % i, [128, 520], BF16) for i in range(8)]
            self.rc = sb("rc", [128, T], F32)
            self.rs = sb("rs", [128, T], F32)
            self.prm = sb("prm_sb", [128, self.nl * NPRM], F32)
            self.rmat = sb("rmat_sb", [128, 128], BF16)
            self.o128 = sb("o128", [128, 128], BF16)
            self.o2048 = sb("o2048", [128, 128], BF16)
            self.obf = sb("obf", [128, 128], BF16)
            self.maskt = sb("maskt", [128, 2], F32)
            self.hedge = sb("hedge", [128, KC * 10], BF16)
            self.halo_st = sb("halo_st", [128, 32], BF16)
            self.halo_out = sb("halo_out", [128, 32], BF16)
            self.epsb = sb("epsb", [128, 1], F32)
            self.rstd_t = sb("rstd_t", [128, T], F32)
            self.uh = sb("uh", [128, KC * 10], F32)
            self.ps = [st.enter_context(nc.psum_tensor("ps%d" % i, [128, 512], F32)) for i in range(8)]
            self.Uf = self.U[:].bitcast(F32)
            P = self.P
            P.add('sp', lambda e: e.dma_start(out=self.prm[:], in_=self.prm_d), writes=['prm'], dma_key='c0')
            P.add('sp', lambda e: e.dma_start(out=self.rmat[:], in_=self.rmat_d), writes=['rmat'], dma_key='c1')
            P.add('sp', lambda e: e.dma_start(out=self.maskt[:], in_=self.mask_d), writes=['mask'], dma_key='c2')
            P.add('dve', lambda e: e.memset(self.o128[:], 1.0 / 128.0), writes=['o128'])
            P.add('dve', lambda e: e.memset(self.o2048[:], 1.0 / 2048.0), writes=['o2048'])
            P.add('dve', lambda e: e.memset(self.obf[:], 1.0), writes=['obf'])
            P.add('dve', lambda e: e.memset(self.hedge[:], 0.0), writes=['hedge'])
            P.add('dve', lambda e: e.memset(self.epsb[:], EPS), writes=['epsb'])
            if WGATHER:
                allc = [list(range(8))]
                for gi in range(self.nl * NGS):
                    r0, r1 = gi * 128, (gi + 1) * 128
                    src = self.wsh.ap()[r0:r1, :]
                    dst = self.wbn.ap()[r0:r1, :]
                    P.add('sp', lambda e, dst=dst, src=src: e.dma_start(out=dst, in_=src),
                          writes=[('wbn', gi)], dma_key=self.dk('wbn', 4))
                for pi in range(self.nl * NPART):
                    l = pi // NPART
                    r0, r1 = pi * 7 * 128, (pi + 1) * 7 * 128
                    a = self.wbn.ap()[r0:r1, :]
                    b = self.wg[pi].ap()
                    P.add('pool', lambda e, a=a, b=b: e.collective_compute("AllGather", ALU.bypass, replica_groups=allc,
                                                                           ins=[a], outs=[b]),
                          reads=[('wbn', pi * 7 + i) for i in range(7)], writes=[('wg', pi)],
                          dma_key='cc', dma_inc=1)
            for l in range(self.nl):
                xa = self.xin if l == 0 else self.xs
                xa_name = 'xin' if l == 0 else 'xs'
                xc = self.out if l == self.nl - 1 else self.xs
                xc_name = 'out' if l == self.nl - 1 else 'xs'
                self.phaseA(l, xa, xa_name)
                self.exchange(l)
                for t in range(NT):
                    self.phaseB(l, t, xa, xa_name)
                    self.phaseC(l, t, xc, xc_name, last=(l == self.nl - 1))
            if self.dry:
                return self.wrec
            P.emit(nc, final_waits=self.fin_ops)
        return nc

    def pcol(self, l, c, n=1):
        return self.prm[:, l * NPRM + c: l * NPRM + c + n]

    def norm_stats(self, src_chunks, src_tokens, n_chunks, omat, omat_tok):
        accb = 7
        for j in range(n_chunks):
            q = self.tb()
            self.act(self.tba(q), src_chunks[j], AF.Square, reads=[src_tokens[j]], writes=[('tb', q)])
            self.mm(self.psa(accb), omat, self.tba(q), j == 0, j == n_chunks - 1,
                    reads=[omat_tok, ('tb', q)], writes=[('ps', accb)])
        return self.rstd_from(accb, dedicated=True)

    def rstd_from(self, bank, dedicated=False):
        q = self.tf()
        qa = self.tfa(q)
        self.act(qa, self.psa(bank), AF.Sqrt, reads=[('ps', bank), 'epsb'], writes=[('tf', q)],
                 bias=self.epsb[:], scale=1.0)
        if dedicated:
            ra, rtok = self.rstd_t[:], 'rstd_t'
        else:
            r = self.tf()
            ra, rtok = self.tfa(r), ('tf', r)
        self.dve(lambda e: e.reciprocal(out=ra, in_=qa), reads=[('tf', q)], writes=[rtok])
        return ra, rtok

    def load_x_and_norm(self, l, t, xsrc, xname, gcol):
        P = self.P
        self.load_xT(xsrc, xname, t)
        self.norm_from_xT(l, gcol)

    def load_xT(self, xsrc, xname, t):
        for k in range(KC):
            src = xsrc[t][:, k * T:(k + 1) * T]
            dst = self.xT[:, k * T:(k + 1) * T]
            self.P.add('sp', lambda e, dst=dst, src=src: e.dma_start(out=dst, in_=src), reads=[(xname, t, k)],
                       writes=[('xT', k)], dma_key=self.dk('xl', 8))

    def norm_from_xT(self, l, gcol):
        xch = [self.xT[:, k * T:(k + 1) * T] for k in range(KC)]
        xtok = [('xT', k) for k in range(KC)]
        rst, rtok = self.norm_stats(xch, xtok, KC, self.o2048[:], 'o2048')
        for k in range(KC):
            o = self.hT[:, k * T:(k + 1) * T]
            i0 = xch[k]
            g = self.pcol(l, gcol + k)
            self.dve(lambda e, o=o, i0=i0, g=g: e.scalar_tensor_tensor(out=o, in0=i0, scalar=g, in1=rst,
                                                                      op0=ALU.mult, op1=ALU.mult),
                     reads=[('xT', k), rtok, 'prm'], writes=[('hT', k)])

    def qk_stages(self, l, bank, gcol, dst, dst_tok, after=None):
        raw = self.psa(bank)
        stt = {}

        def s1():
            q = self.tb()
            stt['q'] = q
            self.act(self.tba(q), raw, AF.Square, reads=[('ps', bank)], writes=[('tb', q)])

        def s2():
            q = stt['q']
            b2 = self.bank()
            self.mm(self.psa(b2), self.o128[:], self.tba(q), True, True, reads=['o128', ('tb', q)], writes=[('ps', b2)])
            rst, rtok = self.rstd_from(b2)
            xn = self.tb()
            stt['xn'] = xn
            xna = self.tba(xn)
            g = self.pcol(l, gcol)
            self.dve(lambda e: e.scalar_tensor_tensor(out=xna, in0=raw, scalar=g, in1=rst, op0=ALU.mult, op1=ALU.mult),
                     reads=[('ps', bank), rtok, 'prm'], writes=[('tb', xn)])

        def s3():
            xn = stt['xn']
            xna = self.tba(xn)
            b3 = self.bank()
            self.mm(self.psa(b3), self.rmat[:], xna, True, True, reads=['rmat', ('tb', xn)], writes=[('ps', b3)])
            t1 = self.tf()
            t1a = self.tfa(t1)
            rc = self.rc[:]
            rs = self.rs[:]
            self.P.add('pool', lambda e: e.tensor_tensor(out=t1a, in0=xna, in1=rc, op=ALU.mult),
                       reads=[('tb', xn), 'rc'], writes=[('tf', t1)])
            t2 = self.tf()
            t2a = self.tfa(t2)
            rx = self.psa(b3)
            self.dve(lambda e: e.tensor_tensor(out=t2a, in0=rx, in1=rs, op=ALU.mult),
                     reads=[('ps', b3), 'rs'], writes=[('tf', t2)])
            self.dve(lambda e: e.tensor_tensor(out=dst, in0=t1a, in1=t2a, op=ALU.add),
                     reads=[('tf', t1), ('tf', t2)], writes=dst_tok)
            if after is not None:
                after()
        return [s1, s2, s3]

    def pipeline(self, mains):
        pend = []

        def advance():
            for p in list(pend):
                p[0][p[1]]()
                p[1] += 1
                if p[1] == len(p[0]):
                    pend.remove(p)
        for m in mains:
            pend.append([m(), 0])
            advance()
        while pend:
            advance()

    def load_rope(self, t):
        c = self.ropec_d[:, t * T:(t + 1) * T]
        s = self.ropes_d[:, t * T:(t + 1) * T]
        self.P.add('sp', lambda e: e.dma_start(out=self.rc[:], in_=c), writes=['rc'], dma_key='rpc')
        self.P.add('sp', lambda e: e.dma_start(out=self.rs[:], in_=s), writes=['rs'], dma_key='rps')

    def phaseA(self, l, xa, xa_name):
        P = self.P
        hch = [self.hT[:, k * T:(k + 1) * T] for k in range(KC)]
        htok = [('hT', k) for k in range(KC)]
        for t in range(NT):
            self.load_rope(t)
            self.load_x_and_norm(l, t, xa, xa_name, 0)
            ho = bass.AP(self.hedge, 1 + 2 * t, [[KC * 10, 128], [10, KC], [1, 2]])
            hi = bass.AP(self.hT, 0, [[KC * T, 128], [T, KC], [T - 1, 2]])
            P.add('act', lambda e, ho=ho, hi=hi: e.copy(out=ho, in_=hi), reads=htok, writes=['hedge'])
            def kmain(h, t=t):
                def m():
                    s_ = self.wload(l * NG + h)
                    bank = self.proj_block(s_, hch, htok)
                    kq = self.tb()

                    def store():
                        dst = self.kin.ap()[h * 128:(h + 1) * 128, t * T:(t + 1) * T]
                        src = self.tba(kq)
                        P.add('sp', lambda e, dst=dst, src=src: e.dma_start(out=dst, in_=src), reads=[('tb', kq)],
                              writes=[('kin', h, t)], dma_key=self.dk('kst', 4))
                    return self.qk_stages(l, bank, 97, self.tba(kq), [('tb', kq)], after=store)
                return m
            self.pipeline([kmain(h) for h in range(NKV)])
            vbanks = [self.bank() for _ in range(4)]
            for vb in range(4):
                s = self.wload(l * NG + 4 + vb)
                for tbk in range(4):
                    for kc in range(KC):
                        self.mm(self.ps[vbanks[tbk]][:, vb * 128:(vb + 1) * 128],
                                self.hT[:, kc * T + tbk * 128: kc * T + (tbk + 1) * 128], self.wk(s, kc),
                                kc == 0, kc == KC - 1,
                                reads=[('wb', s), ('hT', kc)], writes=[('ps', vbanks[tbk])])
            for tbk in range(4):
                vq = self.tb()
                self.act(self.tba(vq), self.psa(vbanks[tbk]), AF.Copy, reads=[('ps', vbanks[tbk])], writes=[('tb', vq)])
                cl = t * 4 + tbk
                dst = bass.AP(self.vin, cl * 128, [[TOK, 128], [128 * TOK, NKV], [1, 128]])
                src = bass.AP(self.tmpb[vq], 0, [[520, 128], [128, NKV], [1, 128]])
                P.add('sp', lambda e, dst=dst, src=src: e.dma_start(out=dst, in_=src), reads=[('tb', vq)],
                      writes=[('vin', tbk, t)], dma_key=self.dk('vst', 4))
        for i, col in enumerate((1, 8)):
            hs = bass.AP(self.hedge, col, [[KC * 10, 128], [10, KC]])
            ho_ = self.halo_out[:, i * 16:(i + 1) * 16]
            P.add('act', lambda e, ho_=ho_, hs=hs: e.copy(out=ho_, in_=hs), reads=['hedge'], writes=[('halo_out', i)])
            dst = self.hin.ap()[i * 128:(i + 1) * 128, :]
            P.add('sp', lambda e, dst=dst, ho_=ho_: e.dma_start(out=dst, in_=ho_),
                  reads=[('halo_out', i)], writes=[('hin', i)], dma_key=('hst', i))

    def exchange(self, l):
        P = self.P
        rg = [[0, 1], [2, 3], [4, 5], [6, 7]]
        intok = {'k': [('kin', h, t) for h in range(NKV) for t in range(NT)],
                 'v': [('vin', tbk, t) for tbk in range(4) for t in range(NT)],
                 'h': [('hin', 0), ('hin', 1)]}
        for name, a, b in (('k', self.kin, self.kout), ('v', self.vin, self.vout), ('h', self.hin, self.hout)):
            P.add('pool', lambda e, a=a, b=b: e.collective_compute("AllGather", ALU.bypass, replica_groups=rg,
                                                                   ins=[a.ap()], outs=[b.ap()]),
                  reads=intok[name], writes=[name + 'out'], dma_key='cc', dma_inc=1)
        for i, r0 in enumerate((128, 256)):
            src = self.hout.ap()[r0:r0 + 128, :]
            dst = self.halo_st[:, i * 16:(i + 1) * 16]
            P.add('sp', lambda e, dst=dst, src=src: e.dma_start(out=dst, in_=src), reads=['hout'],
                  writes=[('halo_st', i)], dma_key=('hld', i))
        for i, col in enumerate((0, 9)):
            o = bass.AP(self.hedge, col, [[KC * 10, 128], [10, KC]])
            i0 = self.halo_st[:, i * 16:(i + 1) * 16]
            m = self.maskt[:, i:i + 1]
            self.dve(lambda e, o=o, i0=i0, m=m: e.tensor_scalar(out=o, in0=i0, scalar1=m, scalar2=None, op0=ALU.mult),
                     reads=[('halo_st', i), 'mask', 'hedge'], writes=['hedge'])

    def Bt(self, lo, n):
        return [('B', i) for i in range(lo, lo + n)]

    def phaseB(self, l, t, xa, xa_name):
        P = self.P
        g0 = l * NG
        hch = [self.hT[:, k * T:(k + 1) * T] for k in range(KC)]
        htok = [('hT', k) for k in range(KC)]
        self.load_rope(t)
        self.load_x_and_norm(l, t, xa, xa_name, 0)
        dbg = (l == 0 and t == 0)
        if dbg:
            self.dump('hT', self.hT[:], htok, KC * T, BF16)
        cbch = [self.U[:, k * T:(k + 1) * T] for k in range(KC)]
        cbtok = [('U', k) for k in range(KC)]
        for j in range(KC):
            s = self.wload(g0 + 8 + 3 * j)
            bank = self.proj_block(s, hch, htok)
            if t == 0:
                hb = self.bank()
                for kc in range(KC):
                    self.mm(self.ps[hb][:, 0:10], self.wk(s, kc), self.hedge[:, kc * 10:(kc + 1) * 10],
                            kc == 0, kc == KC - 1, reads=[('wb', s), 'hedge'], writes=[('ps', hb)])
                ch = self.tf()
                self.act(self.tfa(ch, 10), self.ps[hb][:, 0:10], AF.Copy, reads=[('ps', hb)], writes=[('tf', ch)])
            ce = self.tf()
            self.act(self.tfa(ce), self.psa(bank), AF.Copy, reads=[('ps', bank)], writes=[('tf', ce)])
            s = self.wload(g0 + 8 + 3 * j + 1)
            bank = self.proj_block(s, hch, htok)
            if t == 0:
                hb = self.bank()
                for kc in range(KC):
                    self.mm(self.ps[hb][:, 0:10], self.wk(s, kc), self.hedge[:, kc * 10:(kc + 1) * 10],
                            kc == 0, kc == KC - 1, reads=[('wb', s), 'hedge'], writes=[('ps', hb)])
                uho = self.uh[:, j * 10:(j + 1) * 10]
                cha = self.tfa(ch, 10)
                hsrc = self.ps[hb][:, 0:10]
                self.dve(lambda e, uho=uho, cha=cha, hsrc=hsrc: e.tensor_tensor(out=uho, in0=hsrc, in1=cha, op=ALU.mult),
                         reads=[('ps', hb), ('tf', ch)], writes=[('uh', j)])
            ue = self.tf()
            uo = self.tfa(ue, T, 1)
            ci = self.tfa(ce)
            pin = self.psa(bank)
            self.dve(lambda e, uo=uo, ci=ci, pin=pin: e.tensor_tensor(out=uo, in0=pin, in1=ci, op=ALU.mult),
                     reads=[('ps', bank), ('tf', ce)], writes=[('tf', ue)])
            ueo = bass.AP(self.tmpf[ue], 0, [[520, 128], [T + 1, 2]])
            uhi = bass.AP(self.uh, j * 10 + 2 * t, [[KC * 10, 128], [3, 2]])
            P.add('act', lambda e, ueo=ueo, uhi=uhi: e.copy(out=ueo, in_=uhi), reads=[('uh', j)], writes=[('tf', ue)])
            c1 = self.tf()
            self.act(self.tfa(c1), self.tfa(ue, T, 1), AF.Copy, reads=[('tf', ue), 'prm'], writes=[('tf', c1)],
                     scale=self.pcol(l, 48 + 16 + j))
            c2 = self.tf()
            a0 = self.tfa(ue, T, 0)
            a2 = self.tfa(ue, T, 2)
            w0 = self.pcol(l, 48 + j)
            w2 = self.pcol(l, 48 + 32 + j)
            c1a = self.tfa(c1)
            c2a = self.tfa(c2)
            self.dve(lambda e, a0=a0, w0=w0, c1a=c1a, c2a=c2a: e.scalar_tensor_tensor(
                out=c2a, in0=a0, scalar=w0, in1=c1a, op0=ALU.mult, op1=ALU.add),
                reads=[('tf', ue), ('tf', c1), 'prm'], writes=[('tf', c2)])
            c3 = self.tf()
            c3a = self.tfa(c3)
            self.dve(lambda e, a2=a2, w2=w2, c2a=c2a, c3a=c3a: e.scalar_tensor_tensor(
                out=c3a, in0=a2, scalar=w2, in1=c2a, op0=ALU.mult, op1=ALU.add),
                reads=[('tf', ue), ('tf', c2), 'prm'], writes=[('tf', c3)])
            s = self.wload(g0 + 8 + 3 * j + 2)
            bank = self.proj_block(s, hch, htok)
            pb = self.psa(bank)
            o = cbch[j]
            self.dve(lambda e, o=o, pb=pb, c3a=c3a: e.tensor_tensor(out=o, in0=pb, in1=c3a, op=ALU.mult),
                     reads=[('ps', bank), ('tf', c3)], writes=[cbtok[j]])
        QB = 32
        qch = [self.BIG[:, (QB + h) * T:(QB + h + 1) * T] for h in range(NQ)]
        qtok = [('B', QB + h) for h in range(NQ)]
        def qmain(h):
            def m():
                s_ = self.wload(g0 + 56 + h)
                bank = self.proj_block(s_, hch, htok)
                return self.qk_stages(l, bank, 96, qch[h], [qtok[h]])
            return m
        self.pipeline([qmain(h) for h in range(NQ)])
        if dbg:
            self.dump('cbT', self.U[:, 0:KC * T], cbtok, KC * T, BF16)
            self.dump('qT', self.BIG[:, QB * T:(QB + NQ) * T], qtok, KC * T, BF16)
        self.attention(l, t, qch, qtok)
        if dbg:
            self.dump('atT', self.U[:, KC * T:2 * KC * T], [('U', KC + k) for k in range(KC)], KC * T, BF16)
        atch = [self.U[:, (KC + k) * T:(KC + k + 1) * T] for k in range(KC)]
        attok = [('U', KC + k) for k in range(KC)]
        mch = qch
        mtok = qtok
        for j in range(KC):
            s = self.wload(g0 + 72 + 4 * j + 2)
            bga = self.proj_block(s, hch, htok)
            s = self.wload(g0 + 72 + 4 * j + 3)
            bgb = self.proj_block(s, hch, htok)
            s = self.wload(g0 + 72 + 4 * j)
            bya = self.proj_block(s, cbch, cbtok)
            s = self.wload(g0 + 72 + 4 * j + 1)
            byb = self.proj_block(s, atch, attok)
            sa = self.tf()
            self.act(self.tfa(sa), self.psa(bga), AF.Sigmoid, reads=[('ps', bga), 'prm'], writes=[('tf', sa)],
                     bias=self.pcol(l, 16 + j))
            sbb = self.tf()
            self.act(self.tfa(sbb), self.psa(bgb), AF.Sigmoid, reads=[('ps', bgb), 'prm'], writes=[('tf', sbb)],
                     bias=self.pcol(l, 32 + j))
            m1 = self.tf()
            m1a, saa, sba = self.tfa(m1), self.tfa(sa), self.tfa(sbb)
            pya, pyb = self.psa(bya), self.psa(byb)
            self.dve(lambda e, m1a=m1a, pya=pya, saa=saa: e.tensor_tensor(out=m1a, in0=pya, in1=saa, op=ALU.mult),
                     reads=[('ps', bya), ('tf', sa)], writes=[('tf', m1)])
            m2 = self.tf()
            m2a = self.tfa(m2)
            self.dve(lambda e, m2a=m2a, pyb=pyb, sba=sba: e.tensor_tensor(out=m2a, in0=pyb, in1=sba, op=ALU.mult),
                     reads=[('ps', byb), ('tf', sbb)], writes=[('tf', m2)])
            o = mch[j]
            P.add('pool', lambda e, o=o, m1a=m1a, m2a=m2a: e.tensor_tensor(out=o, in0=m1a, in1=m2a, op=ALU.add),
                  reads=[('tf', m1), ('tf', m2)], writes=[mtok[j]])
        if dbg:
            self.dump('mT', self.BIG[:, QB * T:(QB + NQ) * T], mtok, KC * T, BF16)
        self.proj_norm_residual(l, g0 + 136, 1, mch, mtok, 98, None, 'xmid', t)

    def proj_norm_residual(self, l, gbase, gper, rch, rtok, gcol, xdst, xdst_name, t):
        P = self.P
        fch = [self.Uf[:, j * T:(j + 1) * T] for j in range(KC)]
        ftok = [[('U', 2 * j), ('U', 2 * j + 1)] for j in range(KC)]
        accb = 7
        pend = None

        def ones(jq):
            jj, q = jq
            self.mm(self.psa(accb), self.o2048[:], self.tba(q), jj == 0, jj == KC - 1,
                    reads=['o2048', ('tb', q)], writes=[('ps', accb)])
        for j in range(KC):
            bank = self.bank()
            for gi in range(gper):
                s = self.wload(gbase + j * gper + gi)
                self.proj_block(s, rch, rtok, bank=bank, start=(gi == 0), stop=(gi == gper - 1), kbase=gi * KC)
            q = self.tb()
            self.act(self.tba(q), self.psa(bank), AF.Square, reads=[('ps', bank)], writes=[('tb', q)])
            self.act(fch[j], self.psa(bank), AF.Copy, reads=[('ps', bank)], writes=ftok[j])
            if pend is not None:
                ones(pend)
            pend = (j, q)
        ones(pend)
        rst, rtok = self.rstd_from(accb, dedicated=True)
        for j in range(KC):
            tm = self.tf()
            tma = self.tfa(tm)
            g = self.pcol(l, gcol + j)
            fj = fch[j]
            self.dve(lambda e, tma=tma, fj=fj, g=g: e.scalar_tensor_tensor(out=tma, in0=fj, scalar=g, in1=rst,
                                                                          op0=ALU.mult, op1=ALU.mult),
                     reads=ftok[j] + [rtok, 'prm'], writes=[('tf', tm)])
            xj = self.xT[:, j * T:(j + 1) * T]
            P.add('pool', lambda e, xj=xj, tma=tma: e.tensor_tensor(out=xj, in0=xj, in1=tma, op=ALU.add),
                  reads=[('xT', j), ('tf', tm)], writes=[('xT', j)])
            if xdst is None:
                continue
            dst = xdst[t][:, j * T:(j + 1) * T]
            op = P.add('sp', lambda e, dst=dst, xj=xj: e.dma_start(out=dst, in_=xj), reads=[('xT', j)],
                       writes=[(xdst_name, t, j)], dma_key=self.dk('xst', 8))
            if xdst_name == 'out':
                self.fin_ops.append(op)
        if l == 0 and t == 0 and xdst_name == 'xmid':
            self.dump('xm', self.xT[:], [('xT', k) for k in range(KC)], KC * T, F32)

    def attention(self, l, t, qch, qtok):
        P = self.P
        def kslot(i):
            return self.BIG[:, i * 8 * T:(i + 1) * 8 * T]

        def vslot(i):
            return self.BIG[:, (16 + i * 8) * T:(16 + (i + 1) * 8) * T]
        accf = self.BIG[:, 52 * T:56 * T].bitcast(F32)
        for kvh in range(NKV):
            ks = kvh % 2
            ktok = self.Bt(ks * 8, 8)
            vtok = self.Bt(16 + ks * 8, 8)
            for r in range(2):
                src = self.kout.ap()[r * 512 + kvh * 128: r * 512 + (kvh + 1) * 128, :]
                dst = self.BIG[:, (ks * 8) * T + r * TOK:(ks * 8) * T + (r + 1) * TOK]
                P.add('sp', lambda e, dst=dst, src=src: e.dma_start(out=dst, in_=src), reads=['kout'],
                      writes=self.Bt(ks * 8 + 4 * r, 4), dma_key=('kl', ks, r))
            for r in range(2):
                src = self.vout.ap()[r * 512 + kvh * 128: r * 512 + (kvh + 1) * 128, :]
                dst = self.BIG[:, (16 + ks * 8) * T + r * TOK:(16 + ks * 8) * T + (r + 1) * TOK]
                P.add('sp', lambda e, dst=dst, src=src: e.dma_start(out=dst, in_=src), reads=['vout'],
                      writes=self.Bt(16 + ks * 8 + 4 * r, 4), dma_key=('vl', ks, r))
            if l == 0 and t == 0 and kvh == 0:
                self.dump('K0', self.BIG[:, 0:8 * T], self.Bt(0, 8), 8 * T, BF16)
                self.dump('V0', self.BIG[:, 16 * T:24 * T], self.Bt(16, 8), 8 * T, BF16)
            for hq in range(4):
                h = kvh * 4 + hq
                self.convert_some(l + 1, 5 if h % 8 < 3 else 4)
                ob = 6 + (h % 2)
                ai = h % 2
                acc = accf[:, ai * T:(ai + 1) * T]
                acctok = self.Bt(52 + 2 * ai, 2)
                NCH = 32
                sbanks = {}
                pts = {}

                dbk = 4 + (h % 2)

                def S(c):
                    b = self.bank_rr % 4
                    self.bank_rr += 1
                    sbanks[c] = b
                    self.mm(self.psa(b), self.BIG[:, ks * 8 * T + c * 128: ks * 8 * T + (c + 1) * 128], qch[h],
                            True, True, reads=self.Bt(ks * 8 + 4 * (c // 16), 4) + [qtok[h]], writes=[('ps', b)])

                def E(c):
                    b = sbanks.pop(c)
                    pi = c % 8
                    pts[c] = pi
                    pt = self.BIG[:, (48 + pi) * T:(49 + pi) * T]
                    self.act(pt, self.psa(b), AF.Exp, reads=[('ps', b)], writes=[('B', 48 + pi)], scale=QSCALE)

                def PV(c):
                    pi = pts[c]
                    pt = self.BIG[:, (48 + pi) * T:(49 + pi) * T]
                    vv = self.BIG[:, (16 + ks * 8) * T + c * 128:(16 + ks * 8) * T + (c + 1) * 128]
                    self.mm(self.psa(ob), vv, pt, c == 0, c == NCH - 1,
                            reads=self.Bt(16 + ks * 8 + 4 * (c // 16), 4) + [('B', 48 + pi)], writes=[('ps', ob)])

                gsum = {}

                def GS(g):
                    p = [self.BIG[:, (48 + pts[4 * g + i]) * T:(49 + pts[4 * g + i]) * T] for i in range(4)]
                    pk = [('B', 48 + pts[4 * g + i]) for i in range(4)]
                    a = self.tf()
                    aa = self.tfa(a)
                    self.dve(lambda e, aa=aa, p=p: e.tensor_tensor(out=aa, in0=p[0], in1=p[1], op=ALU.add),
                             reads=pk[0:2], writes=[('tf', a)])
                    b_ = self.tf()
                    ba = self.tfa(b_)
                    self.dve(lambda e, ba=ba, p=p: e.tensor_tensor(out=ba, in0=p[2], in1=p[3], op=ALU.add),
                             reads=pk[2:4], writes=[('tf', b_)])
                    sg = self.tb()
                    sga = self.tba(sg)
                    self.dve(lambda e, sga=sga, aa=aa, ba=ba: e.tensor_tensor(out=sga, in0=aa, in1=ba, op=ALU.add),
                             reads=[('tf', a), ('tf', b_)], writes=[('tb', sg)])
                    gsum[g] = sg

                def DEN(g):
                    sg = gsum.pop(g)
                    self.mm(self.psa(dbk), self.obf[:], self.tba(sg), g == 0, g == NCH // 4 - 1,
                            reads=['obf', ('tb', sg)], writes=[('ps', dbk)])
                S(0)
                S(1)
                for c in range(NCH):
                    E(c)
                    if c + 2 < NCH:
                        S(c + 2)
                    PV(c)
                    if c % 4 == 3:
                        GS(c // 4)
                    if c % 4 == 1 and c > 4:
                        DEN(c // 4 - 1)
                DEN(NCH // 4 - 1)
                rd = self.tf()
                rda = self.tfa(rd)
                den = self.psa(dbk)
                self.dve(lambda e, rda=rda, den=den: e.reciprocal(out=rda, in_=den), reads=[('ps', dbk)], writes=[('tf', rd)])
                o = self.U[:, (KC + h) * T:(KC + h + 1) * T]
                po = self.psa(ob)
                self.dve(lambda e, o=o, po=po, rda=rda: e.tensor_tensor(out=o, in0=po, in1=rda, op=ALU.mult),
                         reads=[('ps', ob), ('tf', rd)], writes=[('U', KC + h)])

    def phaseC(self, l, t, xc, xc_name, last):
        P = self.P
        g0 = l * NG
        self.norm_from_xT(l, 114)
        hch = [self.hT[:, k * T:(k + 1) * T] for k in range(KC)]
        htok = [('hT', k) for k in range(KC)]
        ach = [self.BIG[:, f * T:(f + 1) * T] for f in range(FC)]
        atok = [('B', f) for f in range(FC)]
        for f in range(FC):
            s = self.wload(g0 + 152 + f)
            bank = self.proj_block(s, hch, htok)
            r = self.tf()
            self.act(self.tfa(r), self.psa(bank), AF.Relu, reads=[('ps', bank)], writes=[('tf', r)])
            ra = self.tfa(r)
            o = ach[f]
            if f % 2 == 0:
                self.dve(lambda e, o=o, ra=ra: e.tensor_tensor(out=o, in0=ra, in1=ra, op=ALU.mult),
                         reads=[('tf', r)], writes=[atok[f]])
            else:
                P.add('pool', lambda e, o=o, ra=ra: e.tensor_tensor(out=o, in0=ra, in1=ra, op=ALU.mult),
                      reads=[('tf', r)], writes=[atok[f]])
        self.proj_norm_residual(l, g0 + 216, 4, ach, atok, 130, xc, xc_name, t)


def _grp(W, cols):
    blk = W[:, cols]
    return blk.reshape(16, 128, 128).transpose(1, 0, 2).reshape(128, 2048)


def _layer_groups(w_in, w_oc, w_oa, w_mg, w_up, w_dn):
    out = np.empty((NG, 128, 2048), np.float32)
    ar = np.arange(128)
    g = 0
    for h in range(4):
        out[g] = _grp(w_in, OFF_K + h * 128 + ar); g += 1
    for vb in range(4):
        out[g] = _grp(w_in, OFF_V + vb * 128 + ar); g += 1
    for j in range(16):
        for base in (OFF_C, OFF_IN, OFF_B):
            out[g] = _grp(w_in, base + j * 128 + ar); g += 1
    for h in range(16):
        out[g] = _grp(w_in, OFF_Q + h * 128 + ar); g += 1
    for j in range(16):
        out[g] = _grp(w_oc, j * 128 + ar); g += 1
        out[g] = _grp(w_oa, j * 128 + ar); g += 1
        out[g] = _grp(w_in, OFF_GA + j * 128 + ar); g += 1
        out[g] = _grp(w_in, OFF_GB + j * 128 + ar); g += 1
    for j in range(16):
        out[g] = _grp(w_mg, j * 128 + ar); g += 1
    for f in range(64):
        out[g] = _grp(w_up, f * 128 + ar); g += 1
    for ob in range(16):
        for q in range(4):
            blk = w_dn[q * 2048:(q + 1) * 2048, ob * 128:(ob + 1) * 128]
            out[g] = blk.reshape(16, 128, 128).transpose(1, 0, 2).reshape(128, 2048); g += 1
    assert g == NG
    return out


def _chunk_cols(v):
    return np.ascontiguousarray(v.reshape(16, 128).T)


def _layer_params(l, inp):
    p = np.empty((128, NPRM), np.float32)
    p[:, 0:16] = _chunk_cols(inp['norm_mix_pre'][l])
    p[:, 16:32] = _chunk_cols(inp['gate_bias'][l][:2048])
    p[:, 32:48] = _chunk_cols(inp['gate_bias'][l][2048:])
    for i in range(3):
        p[:, 48 + 16 * i:64 + 16 * i] = _chunk_cols(inp['conv_w'][l][i])
    p[:, 96] = inp['q_norm'][l]
    p[:, 97] = inp['k_norm'][l]
    p[:, 98:114] = _chunk_cols(inp['norm_mix_post'][l])
    p[:, 114:130] = _chunk_cols(inp['norm_mlp_pre'][l])
    p[:, 130:146] = _chunk_cols(inp['norm_mlp_post'][l])
    return p


def _rope_tables(half):
    tok = np.arange(half * TOK, (half + 1) * TOK)
    row = (tok // 64).astype(np.float32)
    col = (tok % 64).astype(np.float32)
    inv = (10000.0 ** (-np.arange(0, 64, 2, dtype=np.float32) / 64.0)).astype(np.float32)
    d = np.arange(128)
    f = d % 32
    pos = np.where((d // 64)[:, None] == 0, row[None, :], col[None, :]).astype(np.float32)
    ang = (pos * inv[f][:, None]).astype(np.float32)
    return np.cos(ang).astype(np.float32), np.sin(ang).astype(np.float32)


def _rmat():
    m = np.zeros((128, 128), np.float32)
    for d in range(128):
        if (d % 64) < 32:
            m[d + 32, d] = -1.0
        else:
            m[d - 32, d] = 1.0
    return m.astype(ml_dtypes.bfloat16)


def _x_to_tiles(xc):
    return np.ascontiguousarray(xc.reshape(NT, T, KC, 128).transpose(0, 3, 2, 1).reshape(NT, 128, KC * T))


def _tiles_to_x(o):
    return o.reshape(NT, 128, KC, T).transpose(0, 3, 2, 1).reshape(TOK, D)


_NC_CACHE = {}


def _get_nc(nlayers):
    if nlayers not in _NC_CACHE:
        wseq = Builder(nlayers).build()
        _NC_CACHE[nlayers] = Builder(nlayers, wseq).build()
    return _NC_CACHE[nlayers]


def _run_layers(x_tiles, layers, inp, fused_nc=None):
    nl = len(layers)
    nc = _get_nc(nl)
    lg = [_layer_groups(inp['w_in'][l], inp['w_out_conv'][l], inp['w_out_attn'][l],
                        inp['w_merge'][l], inp['w_up'][l], inp['w_down'][l]) for l in layers]
    if not WGATHER:
        wall = np.concatenate(lg, axis=0)
    prm = np.concatenate([_layer_params(l, inp) for l in layers], axis=1)
    rm = _rmat()
    in_maps = []
    for c in range(8):
        half = c % 2
        rc, rs = _rope_tables(half)
        mask = np.zeros((128, 2), np.float32)
        mask[:, 0] = 1.0 if half == 1 else 0.0
        mask[:, 1] = 1.0 if half == 0 else 0.0
        m = {"xin": x_tiles[c], "prm": prm, "ropec": rc, "ropes": rs, "rmat": rm, "mask": mask}
        if WGATHER:
            m["wsh"] = np.concatenate([g[p * NGP + c * 7: p * NGP + (c + 1) * 7] for g in lg for p in range(NPART)],
                                      axis=0).reshape(nl * NGS * 128, 2048)
        else:
            m["wall"] = wall
        in_maps.append(m)
    res = run_bass_kernel_spmd(nc, in_maps, core_ids=list(range(8)))
    if DEBUG:
        global _DBG
        _DBG = res.results
    return [np.asarray(res.results[c]["out"]) for c in range(8)]


FUSED = True


def kernel(**inputs):
    inp = {k: np.asarray(v) for k, v in inputs.items()}
    x = inp['x']
    tiles = []
    for c in range(8):
        b, half = c // 2, c % 2
        tiles.append(_x_to_tiles(x[b, half * TOK:(half + 1) * TOK, :]))
    if FUSED:
        tiles = _run_layers(tiles, list(range(NL)), inp)
    else:
        for l in range(NL):
            tiles = _run_layers(tiles, [l], inp)
    out = np.empty((4, 4096, D), np.float32)
    for c in range(8):
        b, half = c // 2, c % 2
        out[b, half * TOK:(half + 1) * TOK, :] = _tiles_to_x(tiles[c])
    return out
```
